# Optimizing a Trainium2 kernel written in Bass

```python
import jax, jax.numpy as jnp
from jax import lax
import numpy as np

D_MODEL = 1024
BATCH = 16
SEQ = 4096
DEPTH = 1

CTX_LEN = 256
GRID_W = 64
HEAD_DIM = 64
NA_HEADS = 8
RW_HEADS = 8
NA_WIDTH = NA_HEADS * HEAD_DIM
RW_WIDTH = RW_HEADS * HEAD_DIM
D_MIX = NA_WIDTH + RW_WIDTH
NA_KH = 8
NA_KW = 16
DECAY_LORA = 64
AAA_LORA = 64
SHORT_CONV = 3
RMS_EPS = 1e-6
GN_EPS = 64e-5

O_NA_K = 0
O_NA_V = O_NA_K + NA_WIDTH
O_RW_K = O_NA_V + NA_WIDTH
RW_PREP_W = 2 * RW_WIDTH + 2 * DECAY_LORA + 2 * AAA_LORA
O_CTX_END = O_RW_K + RW_PREP_W
O_RW_R = O_CTX_END
O_CONV_END = O_RW_R + RW_WIDTH
O_NA_Q = O_CONV_END
O_NA_G = O_NA_Q + NA_WIDTH
O_RW_G = O_NA_G + NA_WIDTH
D_IN = O_RW_G + RW_WIDTH
CONV_W = O_CONV_END - O_RW_K

kernel_name = 'hybrid_na_rwkv7_dit_block'


def _rmsnorm(x, g):
    xf = x.astype(jnp.float32)
    y = xf * lax.rsqrt(jnp.mean(xf * xf, axis=-1, keepdims=True) + RMS_EPS)
    return (y * g).astype(x.dtype)


def _modulation(cvec, w_mod, b_mod):
    m = jax.nn.silu(cvec) @ w_mod + b_mod
    return jnp.split(m, 3, axis=-1)


def _short_conv(u, w):
    up = jnp.pad(u, ((0, 0), (1, 1), (0, 0)))
    return up[:, :-2] * w[0] + up[:, 1:-1] * w[1] + up[:, 2:] * w[2]


def _na_heads(t):
    B, T, _ = t.shape
    return jnp.transpose(t.reshape(B, T, NA_HEADS, HEAD_DIM), (0, 2, 1, 3))


def _merge(t):
    B, H, T, N = t.shape
    return jnp.transpose(t, (0, 2, 1, 3)).reshape(B, T, H * N)


def _rw_heads(t):
    return t.reshape(t.shape[:-1] + (RW_HEADS, HEAD_DIM))


def _neighborhood_attention(q, k, v, kc, vc, rpb):
    B, H, T, N = q.shape
    rows = T // GRID_W
    kh = min(NA_KH, rows)
    qg = (q * HEAD_DIM ** -0.5).reshape(B, H, rows, GRID_W, N)
    kg = k.reshape(B, H, rows, GRID_W, N)
    vg = v.reshape(B, H, rows, GRID_W, N)
    col = jnp.arange(GRID_W)
    col_idx = jnp.clip(col - NA_KW // 2, 0, GRID_W - NA_KW)[:, None] + jnp.arange(NA_KW)[None, :]
    col_off = col_idx - col[:, None] + (NA_KW - 1)

    def row_block(i):
        r0 = jnp.clip(i - kh // 2, 0, rows - kh)
        kw = jnp.take(lax.dynamic_slice_in_dim(kg, r0, kh, axis=2), col_idx, axis=3)
        vw = jnp.take(lax.dynamic_slice_in_dim(vg, r0, kh, axis=2), col_idx, axis=3)
        qi = lax.dynamic_index_in_dim(qg, i, axis=2, keepdims=False)
        row_off = r0 + jnp.arange(kh) - i + (NA_KH - 1)
        bias = rpb[:, row_off][:, :, col_off]
        s_loc = jnp.einsum('bhqd,bhrqcd->bhqrc', qi, kw) + jnp.transpose(bias, (0, 2, 1, 3))
        s_ctx = jnp.einsum('bhqd,bhld->bhql', qi, kc)
        s = jnp.concatenate([s_loc.reshape(B, H, GRID_W, kh * NA_KW), s_ctx], axis=-1).astype(jnp.float32)
        pr = jax.nn.softmax(s, axis=-1).astype(v.dtype)
        p_loc = pr[..., :kh * NA_KW].reshape(B, H, GRID_W, kh, NA_KW)
        return (jnp.einsum('bhqrc,bhrqcd->bhqd', p_loc, vw)
                + jnp.einsum('bhql,bhld->bhqd', pr[..., kh * NA_KW:], vc))

    out = lax.map(row_block, jnp.arange(rows))
    return jnp.transpose(out, (1, 2, 0, 3, 4)).reshape(B, H, T, N)


def _dense_attention(q, k, v):
    s = jnp.einsum('bhqd,bhkd->bhqk', q, k).astype(jnp.float32) * HEAD_DIM ** -0.5
    pr = jax.nn.softmax(s, axis=-1).astype(v.dtype)
    return jnp.einsum('bhqk,bhkd->bhqd', pr, v)


def _rwkv_prep(u, w0, w2, a0, a2, k_k, k_a):
    B, T, _ = u.shape
    k, v, wd, ad = jnp.split(u, [RW_WIDTH, 2 * RW_WIDTH, 2 * RW_WIDTH + 2 * DECAY_LORA], axis=-1)
    wd = wd.reshape(B, T, 2, DECAY_LORA)
    ad = ad.reshape(B, T, 2, AAA_LORA)
    wlog = -jax.nn.softplus(-(w0 + jnp.einsum('btdr,drc->btdc', jnp.tanh(wd), w2))) - 0.5
    decay = jnp.exp(-jnp.exp(wlog.astype(jnp.float32)))
    a = jax.nn.sigmoid(a0 + jnp.einsum('btdr,drc->btdc', ad, a2))
    kk = _rw_heads(k * k_k).astype(jnp.float32)
    kk = kk * lax.rsqrt(jnp.maximum(jnp.sum(kk * kk, axis=-1, keepdims=True), 1e-24))
    kd = k[:, :, None] * (1.0 + (a - 1.0) * k_a)
    b = kk[:, :, None] * _rw_heads(a)
    return _rw_heads(v), _rw_heads(decay), _rw_heads(kd), kk, b


def _rwkv7_scan(s0, decay, k, kk, b, v, r, reverse):
    tm = lambda t: jnp.moveaxis(t.astype(jnp.float32), 1, 0)
    xs = (tm(decay), tm(k), tm(kk), tm(b), tm(v)) + (() if r is None else (tm(r),))

    def step(s, inp):
        w_t, k_t, kk_t, b_t, v_t = inp[:5]
        sa = jnp.einsum('bhij,bhj->bhi', s, kk_t)
        s = s * w_t[:, :, None, :] - sa[..., None] * b_t[:, :, None, :] + v_t[..., None] * k_t[:, :, None, :]
        y = None if r is None else jnp.einsum('bhij,bhj->bhi', s, inp[5])
        return s, y

    s, ys = lax.scan(step, s0, xs, reverse=reverse)
    return s, (None if r is None else jnp.moveaxis(ys, 0, 1).astype(v.dtype))


def _rwkv_readout(y, r, kd, v, r_k, gn_w, gn_b):
    B, T = y.shape[:2]
    yf = y.astype(jnp.float32)
    mu = jnp.mean(yf, axis=-1, keepdims=True)
    var = jnp.mean(jnp.square(yf - mu), axis=-1, keepdims=True)
    yn = ((yf - mu) * lax.rsqrt(var + GN_EPS)).astype(y.dtype).reshape(B, T, RW_WIDTH) * gn_w + gn_b
    bonus = jnp.sum(jnp.sum(r[:, :, None] * kd * r_k, axis=-1, keepdims=True) * v[:, :, None], axis=2)
    return yn + bonus.reshape(B, T, RW_WIDTH)


def _layer(h, hc, c, c_ctx, w_mod, b_mod, norm_g, w_in, conv_w, decay_w0, decay_w2,
           aaa_a0, aaa_a2, k_k, k_a, r_k, gn_w, gn_b, na_rpb, w_out, update_ctx):
    B = h.shape[0]
    shift, scale, gate = _modulation(c, w_mod, b_mod)
    shift_c, scale_c, gate_c = _modulation(c_ctx, w_mod, b_mod)
    xn = _rmsnorm(h, norm_g) * (1.0 + scale[:, None]) + shift[:, None]
    xc = _rmsnorm(hc, norm_g) * (1.0 + scale_c) + shift_c
    c_end = D_IN if update_ctx else O_CTX_END
    rc_end = O_CONV_END if update_ctx else O_CTX_END
    p = xn @ w_in
    pc = xc @ w_in[:, :c_end]
    rw_args = (decay_w0, decay_w2, aaa_a0, aaa_a2, k_k, k_a)

    q = _na_heads(p[..., O_NA_Q:O_NA_Q + NA_WIDTH])
    k = _na_heads(p[..., O_NA_K:O_NA_K + NA_WIDTH])
    v = _na_heads(p[..., O_NA_V:O_NA_V + NA_WIDTH])
    kc = _na_heads(pc[..., O_NA_K:O_NA_K + NA_WIDTH])
    vc = _na_heads(pc[..., O_NA_V:O_NA_V + NA_WIDTH])
    na = _merge(_neighborhood_attention(q, k, v, kc, vc, na_rpb)) * jax.nn.silu(p[..., O_NA_G:O_NA_G + NA_WIDTH])

    u = _short_conv(p[..., O_RW_K:O_CONV_END], conv_w)
    uc = _short_conv(pc[..., O_RW_K:rc_end], conv_w[:, :rc_end - O_RW_K])
    v_r, dec, kd, kk, bb = _rwkv_prep(u[..., :RW_PREP_W], *rw_args)
    r_r = _rw_heads(u[..., RW_PREP_W:])
    vc_r, dec_c, kd_c, kk_c, bb_c = _rwkv_prep(uc[..., :RW_PREP_W], *rw_args)
    rc_r = _rw_heads(uc[..., RW_PREP_W:]) if update_ctx else None
    s0 = jnp.zeros((B, RW_HEADS, HEAD_DIM, HEAD_DIM), jnp.float32)
    ys, ycs = [], []
    for d, rev in ((0, False), (1, True)):
        s_ctx, y_ctx = _rwkv7_scan(s0, dec_c[:, :, d], kd_c[:, :, d], kk_c, bb_c[:, :, d], vc_r, rc_r, rev)
        _, y = _rwkv7_scan(s_ctx, dec[:, :, d], kd[:, :, d], kk, bb[:, :, d], v_r, r_r, rev)
        ys.append(y)
        ycs.append(y_ctx)
    rw = _rwkv_readout(ys[0] + ys[1], r_r, kd, v_r, r_k, gn_w, gn_b) * jax.nn.silu(p[..., O_RW_G:O_RW_G + RW_WIDTH])

    h = h + gate[:, None] * (jnp.concatenate([na, rw], axis=-1) @ w_out)
    if update_ctx:
        qc = _na_heads(pc[..., O_NA_Q:O_NA_Q + NA_WIDTH])
        na_c = _merge(_dense_attention(qc, kc, vc)) * jax.nn.silu(pc[..., O_NA_G:O_NA_G + NA_WIDTH])
        rw_c = _rwkv_readout(ycs[0] + ycs[1], rc_r, kd_c, vc_r, r_k, gn_w, gn_b) * jax.nn.silu(pc[..., O_RW_G:O_RW_G + RW_WIDTH])
        hc = hc + gate_c * (jnp.concatenate([na_c, rw_c], axis=-1) @ w_out)
    return h, hc


def setup_inputs(seed: int = 0) -> dict:
    key = jax.random.key(seed)
    ks = jax.random.split(key, 24)
    f32 = jnp.float32
    nrm = lambda kk, shape, s: s * jax.random.normal(kk, shape, f32)
    x = nrm(ks[0], (BATCH, SEQ, D_MODEL), 1.0)
    c = nrm(ks[1], (BATCH, D_MODEL), 1.0)
    ctx = nrm(ks[2], (BATCH, CTX_LEN, D_MODEL), 1.0)
    c_ctx = nrm(ks[3], (D_MODEL,), 1.0)
    w_mod = nrm(ks[4], (DEPTH, D_MODEL, 3 * D_MODEL), 0.5 * D_MODEL ** -0.5)
    b_mod = nrm(ks[5], (DEPTH, 3 * D_MODEL), 0.01)
    norm_g = 1.0 + nrm(ks[6], (DEPTH, D_MODEL), 0.02)
    w_in = nrm(ks[7], (DEPTH, D_MODEL, D_IN), D_MODEL ** -0.5)
    taps = jnp.array([0.25, 1.0, 0.25], f32)[None, :, None]
    conv_w = taps + nrm(ks[8], (DEPTH, SHORT_CONV, CONV_W), 0.05)
    decay_w0 = jax.random.uniform(ks[9], (DEPTH, 2, RW_WIDTH), f32, -6.0, 0.0)
    decay_w2 = nrm(ks[10], (DEPTH, 2, DECAY_LORA, RW_WIDTH), 0.5 * DECAY_LORA ** -0.5)
    aaa_a0 = nrm(ks[11], (DEPTH, 2, RW_WIDTH), 0.1)
    aaa_a2 = nrm(ks[12], (DEPTH, 2, AAA_LORA, RW_WIDTH), 0.5 * AAA_LORA ** -0.5)
    k_k = 0.85 + nrm(ks[13], (DEPTH, RW_WIDTH), 0.02)
    k_a = 1.0 + nrm(ks[14], (DEPTH, RW_WIDTH), 0.02)
    r_k = nrm(ks[15], (DEPTH, RW_HEADS, HEAD_DIM), 0.1)
    gn_w = 1.0 + nrm(ks[16], (DEPTH, RW_WIDTH), 0.02)
    gn_b = nrm(ks[17], (DEPTH, RW_WIDTH), 0.01)
    na_rpb = nrm(ks[18], (DEPTH, NA_HEADS, 2 * NA_KH - 1, 2 * NA_KW - 1), 0.1)
    w_out = nrm(ks[19], (DEPTH, D_MIX, D_MODEL), D_MIX ** -0.5)
    final_g = 1.0 + nrm(ks[20], (D_MODEL,), 0.02)
    return {'x': x, 'c': c, 'ctx': ctx, 'c_ctx': c_ctx, 'w_mod': w_mod, 'b_mod': b_mod,
            'norm_g': norm_g, 'w_in': w_in, 'conv_w': conv_w, 'decay_w0': decay_w0,
            'decay_w2': decay_w2, 'aaa_a0': aaa_a0, 'aaa_a2': aaa_a2, 'k_k': k_k, 'k_a': k_a,
            'r_k': r_k, 'gn_w': gn_w, 'gn_b': gn_b, 'na_rpb': na_rpb, 'w_out': w_out,
            'final_g': final_g}


def reference(x, c, ctx, c_ctx, w_mod, b_mod, norm_g, w_in, conv_w, decay_w0, decay_w2,
              aaa_a0, aaa_a2, k_k, k_a, r_k, gn_w, gn_b, na_rpb, w_out, final_g):
    h, hc = x, ctx
    for l in range(DEPTH):
        h, hc = _layer(h, hc, c, c_ctx, w_mod[l], b_mod[l], norm_g[l], w_in[l], conv_w[l],
                       decay_w0[l], decay_w2[l], aaa_a0[l], aaa_a2[l], k_k[l], k_a[l], r_k[l],
                       gn_w[l], gn_b[l], na_rpb[l], w_out[l], update_ctx=(l < DEPTH - 1))
    return _rmsnorm(h, final_g)
```

```python
import numpy as np
from contextlib import ExitStack
import ml_dtypes
import concourse.bass as bass
import concourse.mybir as mybir
from concourse.bass_utils import run_bass_kernel_spmd

F32 = mybir.dt.float32
BF16 = mybir.dt.bfloat16
AF = mybir.ActivationFunctionType
ALU = mybir.AluOpType
AX = mybir.AxisListType

D = 1024
T = 4096
L = 256
TT = T + L
DIN = 4352
NB = 2
C0 = float(np.exp(-0.5))
RMS_EPS = 1e-6
GN_EPS = 64e-5
NDS = 48
NCHUNK = TT // 64
SN = 128
O_V = 512
O_RW = 1024
O_Q = 2816
O_GNA = 3328
O_GRW = 3840

DEBUG = False
STAGES = ("s0", "s1", "s3", "s2a", "s2b", "s4")


class Buf:
    __slots__ = ("t", "w", "r", "name")

    def __init__(self, t=None, name=""):
        self.t = t
        self.w = None
        self.r = {}
        self.name = name

    def __getitem__(self, k):
        return self.t[k]


class Sched:
    def __init__(self, nc, stack):
        self.nc = nc
        self.E = {"pe": nc.tensor, "act": nc.scalar, "dve": nc.vector, "pool": nc.gpsimd, "sp": nc.sync}
        self.sem = {e: stack.enter_context(nc.semaphore("s_" + e)) for e in self.E}
        self.cnt = {e: 0 for e in self.E}
        self.seen = {e: {} for e in self.E}
        self.dsem = [stack.enter_context(nc.semaphore("d%d" % i)) for i in range(NDS)]
        self.dcnt = [0] * NDS
        self.dnext = 0
        self.uid = 0
        self.dq = 0

    def sb(self, stack, shape, dt, name):
        self.uid += 1
        nm = "%s_%d" % (name, self.uid)
        return Buf(stack.enter_context(self.nc.sbuf_tensor(nm, list(shape), dt)), nm)

    def ps(self, stack, shape, dt, name):
        self.uid += 1
        nm = "%s_%d" % (name, self.uid)
        return Buf(stack.enter_context(self.nc.psum_tensor(nm, list(shape), dt)), nm)

    def _wait(self, e, evs):
        best = {}
        for ev in evs:
            if ev is None:
                continue
            k = id(ev[0])
            if k not in best or best[k][1] < ev[1]:
                best[k] = ev
        seen = self.seen[e]
        for k, (sem, val) in best.items():
            if seen.get(k, 0) < val:
                self.E[e].wait_ge(sem, val)
                seen[k] = val

    def _deps(self, e, reads, writes):
        deps = []
        own = id(self.sem[e])
        for b in reads:
            if b.w is not None and not (e == "pe" and id(b.w[0]) == own):
                deps.append(b.w)
        for b in writes:
            if b.w is not None and not (e == "pe" and id(b.w[0]) == own):
                deps.append(b.w)
            for k, ev in b.r.items():
                if k != own:
                    deps.append(ev)
        return deps

    def _commit(self, ev, reads, writes):
        k = id(ev[0])
        for b in reads:
            b.r[k] = ev
        for b in writes:
            b.w = ev
            b.r = {}

    def op(self, e, fn, reads=(), writes=()):
        self._wait(e, self._deps(e, reads, writes))
        ins = fn(self.E[e])
        self.cnt[e] += 1
        ins.then_inc(self.sem[e], 1)
        ev = (self.sem[e], self.cnt[e])
        self._commit(ev, reads, writes)
        return ev

    def dma(self, out_ap, in_ap, reads=(), writes=(), q=None, **kw):
        if q is None:
            q = "sp"
        i = self.dnext
        self.dnext = (i + 1) % NDS
        deps = self._deps(q, reads, writes)
        if self.dcnt[i] > 0:
            deps.append((self.dsem[i], self.dcnt[i]))
        self._wait(q, deps)
        self.E[q].dma_start(out=out_ap, in_=in_ap, **kw).then_inc(self.dsem[i], 16)
        self.dcnt[i] += 16
        ev = (self.dsem[i], self.dcnt[i])
        self._commit(ev, reads, writes)
        return ev

    def barrier(self):
        evs = [(self.sem[e], self.cnt[e]) for e in self.E if self.cnt[e] > 0]
        evs += [(self.dsem[i], self.dcnt[i]) for i in range(NDS) if self.dcnt[i] > 0]
        for e in ("sp", "act", "dve", "pool", "pe"):
            self._wait(e, evs)


class PsRing:
    def __init__(self, S, stack, n, shape, dt, name):
        self.bufs = [S.ps(stack, shape, dt, name) for _ in range(n)]
        self.i = 0

    def get(self):
        b = self.bufs[self.i]
        self.i = (self.i + 1) % len(self.bufs)
        return b


class SbRing(PsRing):
    def __init__(self, S, stack, n, shape, dt, name):
        self.bufs = [S.sb(stack, shape, dt, name) for _ in range(n)]
        self.i = 0


def bc(ap, shape):
    return ap.to_broadcast(list(shape))


def build_program(stages=STAGES, debug=False, dbg_names=()):
    nc = bass.Bass("TRN2", target_bir_lowering=False)

    def din(name, shape, dt=F32):
        return nc.dram_tensor(name, list(shape), dt, kind="ExternalInput").ap()

    def dscr(name, shape, dt=F32):
        kind = "ExternalOutput" if (debug and name in dbg_names) else "Internal"
        return nc.dram_tensor(name, list(shape), dt, kind=kind).ap()

    x_d = din("x", [NB, T, D])
    ctx_d = din("ctx", [NB, L, D])
    cT_d = din("cT", [128, 8, 3])
    wmod_d = din("w_mod", [D, 3 * D])
    bmodT_d = din("bmodT", [128, 24])
    bgate_d = din("bgate", [3, D])
    sel_d = din("sel", [3, NB, 128])
    normgT_d = din("normgT", [128, 8])
    win_d = din("w_in", [D, DIN])
    convT_d = din("convT", [128, 14, 3])
    dw0T_d = din("dw0T", [128, 2, 4])
    a0T_d = din("a0T", [128, 2, 4])
    dw2_d = din("dw2", [128, 512])
    aw2_d = din("aw2", [128, 512])
    kkT_d = din("kkT", [128, 4])
    kaT_d = din("kaT", [128, 4])
    rkT_d = din("rkT", [128, 4])
    gnw_d = din("gnw_bc", [128, 512])
    gnb_d = din("gnb_bc", [128, 512])
    fg_d = din("fg_bc", [128, D])
    wout_d = din("w_out", [D, D])
    tb_d = din("tb", [128, 8, 14, 64])
    m1_d = din("m1", [128, 2, 2, 64])
    m2_d = din("m2", [128, 2, 2, 64])
    m3_d = din("m3", [128, 2, 64])
    id2_d = din("id2", [128, 64])
    bones_d = din("bones", [128, 128])
    rmask_d = din("rmask", [128, 512])
    identf_d = din("identf", [128, 128])
    out_d = nc.dram_tensor("out", [NB, T, D], F32, kind="ExternalOutput").ap()

    qT_s = dscr("qT_s", [NB, 512, T], BF16)
    kT_s = dscr("kT_s", [NB, 512, TT], BF16)
    v_s = dscr("v_s", [NB, TT, 8 * 65], BF16)
    rwT_s = dscr("rwT_s", [NB, 1792, TT])
    sgna_s = dscr("sgna_s", [NB, T, 512])
    sgrw_s = dscr("sgrw_s", [NB, T, 512])
    nag_s = dscr("nag_s", [NB, T, 512])
    phiy_s = dscr("phiy_s", [NB, 2, NCHUNK, 128, 512], BF16)
    psiy_s = dscr("psiy_s", [NB, 2, NCHUNK, 128, 512])
    yd_s = dscr("yd_s", [NB, 2, T, 512])
    bonus_s = dscr("bonus_s", [NB, T, 512])
    mod_s = dscr("mod_s", [128, 72]) if (debug and "mod_s" in dbg_names) else None

    with ExitStack() as top:
        S = Sched(nc, top)
        psf = PsRing(S, top, 6, [128, 512], F32, "psf")
        psb = PsRing(S, top, 2, [128, 1024], BF16, "psb")

        scale1 = S.sb(top, [128, 8, 3], F32, "scale1")
        shift = S.sb(top, [128, 8, 3], F32, "shift")
        gbc = [S.sb(top, [128, D], F32, "gbc") for _ in range(NB)]
        identf = S.sb(top, [128, 128], F32, "identf")
        identb = S.sb(top, [128, 128], BF16, "identb")
        S.dma(identf[:], identf_d[:, :], writes=[identf])
        S.op("dve", lambda e: e.tensor_copy(identb[:], identf[:]), [identf], [identb])

        if "s0" in stages:
            with ExitStack() as st:
                wm = S.sb(st, [128, 8, 3 * D], F32, "wm")
                for kc in range(8):
                    S.dma(wm[:, kc, :], wmod_d[kc * 128:(kc + 1) * 128, :], writes=[wm],
                          q=("sp" if kc % 2 == 0 else "pool"))
                cT = S.sb(st, [128, 8, 3], F32, "cT")
                sc = S.sb(st, [128, 8, 3], F32, "sc")
                bmodT = S.sb(st, [128, 24], F32, "bmodT")
                normgT = S.sb(st, [128, 8], F32, "normgT")
                bgate = S.sb(st, [3, D], F32, "bgate")
                sel = S.sb(st, [3, NB, 128], F32, "sel")
                modT = S.sb(st, [128, 24, 3], F32, "modT")
                grow = S.sb(st, [3, D], F32, "grow")
                S.dma(cT[:], cT_d[:, :, :], writes=[cT])
                S.dma(bmodT[:], bmodT_d[:, :], writes=[bmodT])
                S.dma(normgT[:], normgT_d[:, :], writes=[normgT])
                S.dma(bgate[:], bgate_d[:, :], writes=[bgate])
                S.dma(sel[:], sel_d[:, :, :], writes=[sel])
                S.op("act", lambda e: e.activation(sc[:], cT[:], AF.Silu), [cT], [sc])
                pm = psf.get()
                for cc in range(24):
                    for kc in range(8):
                        S.op("pe", lambda e, cc=cc, kc=kc: e.matmul(
                            pm[:, cc * 3:(cc + 1) * 3], lhsT=wm[:, kc, cc * 128:(cc + 1) * 128], rhs=sc[:, kc, :],
                            start=(kc == 0), stop=(kc == 7)), [wm, sc], [pm])
                S.op("dve", lambda e: e.tensor_tensor(
                    modT[:], pm[:, 0:72].rearrange("p (c v) -> p c v", v=3),
                    bc(bmodT[:].rearrange("p (c o) -> p c o", o=1), [128, 24, 3]), ALU.add), [pm, bmodT], [modT])
                S.op("dve", lambda e: e.scalar_tensor_tensor(
                    out=scale1[:], in0=modT[:, 8:16, :], scalar=1.0,
                    in1=bc(normgT[:].rearrange("p (c o) -> p c o", o=1), [128, 8, 3]),
                    op0=ALU.add, op1=ALU.mult), [modT, normgT], [scale1])
                S.op("dve", lambda e: e.tensor_copy(shift[:], modT[:, 0:8, :]), [modT], [shift])
                if mod_s is not None:
                    S.dma(mod_s[:, :], modT[:].rearrange("p c v -> p (c v)"), reads=[modT])
                for nh in range(2):
                    pg = psf.get()
                    for kc in range(8):
                        S.op("pe", lambda e, nh=nh, kc=kc, pg=pg: e.matmul(
                            pg[0:3, :], lhsT=sc[:, kc, :], rhs=wm[:, kc, 2 * D + nh * 512:2 * D + (nh + 1) * 512],
                            start=(kc == 0), stop=(kc == 7)), [wm, sc], [pg])
                    S.op("dve", lambda e, nh=nh, pg=pg: e.tensor_tensor(
                        grow[:, nh * 512:(nh + 1) * 512], pg[0:3, :], bgate[:, nh * 512:(nh + 1) * 512], ALU.add),
                        [pg, bgate], [grow])
                for b in range(NB):
                    for nh in range(2):
                        pg = psf.get()
                        S.op("pe", lambda e, b=b, nh=nh, pg=pg: e.matmul(
                            pg[:, :], lhsT=sel[:, b, :], rhs=grow[:, nh * 512:(nh + 1) * 512], start=True, stop=True),
                            [sel, grow], [pg])
                        S.op("act", lambda e, b=b, nh=nh, pg=pg: e.copy(gbc[b][:, nh * 512:(nh + 1) * 512], pg[:, :]), [pg], [gbc[b]])
                S.barrier()

        if "s1" in stages:
            with ExitStack() as st:
                winb = S.sb(st, [128, 8, DIN], BF16, "winb")
                wstg = SbRing(S, st, 2, [128, 1088], F32, "wstg")
                eng_alt = 0
                for kc in range(8):
                    for cq in range(4):
                        ws = wstg.get()
                        S.dma(ws[:], win_d[kc * 128:(kc + 1) * 128, cq * 1088:(cq + 1) * 1088], writes=[ws],
                              q=("sp" if (kc * 4 + cq) % 2 == 0 else "pool"))
                        e_ = "dve" if eng_alt % 2 == 0 else "act"
                        eng_alt += 1
                        if e_ == "dve":
                            S.op("dve", lambda e, kc=kc, cq=cq, ws=ws: e.tensor_copy(
                                winb[:, kc, cq * 1088:(cq + 1) * 1088], ws[:]), [ws], [winb])
                        else:
                            S.op("act", lambda e, kc=kc, cq=cq, ws=ws: e.copy(
                                winb[:, kc, cq * 1088:(cq + 1) * 1088], ws[:]), [ws], [winb])
                xt_r = SbRing(S, st, 2, [128, D], F32, "xt")
                xs_r = SbRing(S, st, 2, [128, D], F32, "xs")
                junk = S.sb(st, [128, D], F32, "junk")
                ss_r = SbRing(S, st, 2, [128, 1], F32, "ss")
                rs_r = SbRing(S, st, 2, [128, 1], F32, "rs")
                xnT_r = SbRing(S, st, 2, [128, 8, 512], BF16, "xnT")
                stgF = SbRing(S, st, 3, [128, 512], F32, "stgF")
                stgB = SbRing(S, st, 3, [128, 512], BF16, "stgB")
                stgV = SbRing(S, st, 2, [128, 8, 65], BF16, "stgV")
                stgG = SbRing(S, st, 3, [128, 512], F32, "stgG")
                for sv in stgV.bufs:
                    S.op("pool", lambda e, sv=sv: e.memset(sv[:], 1.0), [], [sv])
                ev_alt = [0]

                def evac_copy(dst_buf, dst_ap, src_buf, src_ap):
                    ev_alt[0] += 1
                    if ev_alt[0] % 2 == 0:
                        S.op("dve", lambda e: e.tensor_copy(dst_ap, src_ap), [src_buf], [dst_buf])
                    else:
                        S.op("act", lambda e: e.copy(dst_ap, src_ap), [src_buf], [dst_buf])

                for b in range(NB):
                    sts = [(True, 0, 256, 0)] + [(False, i * 512, 512, L + i * 512) for i in range(8)]
                    for (is_ctx, src0, ntok, tok0) in sts:
                        v_idx = 2 if is_ctx else b
                        src = ctx_d if is_ctx else x_d
                        xnT = xnT_r.get()
                        for ti in range(ntok // 128):
                            xt = xt_r.get(); xs = xs_r.get(); ss = ss_r.get(); rs = rs_r.get()
                            S.dma(xt[:], src[b, src0 + ti * 128: src0 + (ti + 1) * 128, :], writes=[xt])
                            S.op("act", lambda e, xt=xt, ss=ss: e.activation(junk[:], xt[:], AF.Square, accum_out=ss[:]),
                                 [xt], [junk, ss])
                            S.op("act", lambda e, ss=ss, rs=rs: e.activation(rs[:], ss[:], AF.Sqrt, bias=RMS_EPS, scale=1.0 / D),
                                 [ss], [rs])
                            S.op("dve", lambda e, rs=rs: e.reciprocal(rs[:], rs[:]), [rs], [rs])
                            S.op("dve", lambda e, xt=xt, xs=xs, rs=rs: e.tensor_scalar(
                                xs[:], xt[:], rs[:, 0:1], None, ALU.mult), [xt, rs], [xs])
                            for half in range(2):
                                pt = psf.get()
                                for j in range(4):
                                    dc = half * 4 + j
                                    S.op("pe", lambda e, pt=pt, j=j, dc=dc, xs=xs: e.transpose(
                                        pt[:, j * 128:(j + 1) * 128], xs[:, dc * 128:(dc + 1) * 128], identf[:]),
                                        [xs, identf], [pt])
                                for j in range(4):
                                    dc = half * 4 + j
                                    if j % 2 == 0:
                                        S.op("act", lambda e, pt=pt, j=j, dc=dc, xnT=xnT, ti=ti: e.activation(
                                            xnT[:, dc, ti * 128:(ti + 1) * 128], pt[:, j * 128:(j + 1) * 128], AF.Identity,
                                            bias=shift[:, dc, v_idx:v_idx + 1], scale=scale1[:, dc, v_idx:v_idx + 1]),
                                            [pt, shift, scale1], [xnT])
                                    else:
                                        S.op("dve", lambda e, pt=pt, j=j, dc=dc, xnT=xnT, ti=ti: e.tensor_scalar(
                                            xnT[:, dc, ti * 128:(ti + 1) * 128], pt[:, j * 128:(j + 1) * 128],
                                            scale1[:, dc, v_idx:v_idx + 1], shift[:, dc, v_idx:v_idx + 1], ALU.mult, ALU.add),
                                            [pt, shift, scale1], [xnT])
                        fm = [("k", i, i * 128) for i in range(4)]
                        fm += [("rw", i, O_RW + i * 128) for i in range(10 if is_ctx else 14)]
                        if not is_ctx:
                            fm += [("q", i, O_Q + i * 128) for i in range(4)]
                        for (kind, i, col0) in fm:
                            pp = psf.get()
                            for kc in range(8):
                                S.op("pe", lambda e, pp=pp, kc=kc, col0=col0, xnT=xnT: e.matmul(
                                    pp[:, 0:ntok], lhsT=winb[:, kc, col0:col0 + 128], rhs=xnT[:, kc, 0:ntok],
                                    start=(kc == 0), stop=(kc == 7)), [winb, xnT], [pp])
                            if kind == "rw":
                                sg = stgF.get()
                                evac_copy(sg, sg[:, 0:ntok], pp, pp[:, 0:ntok])
                                S.dma(rwT_s[b, i * 128:(i + 1) * 128, tok0:tok0 + ntok], sg[:, 0:ntok], reads=[sg], q="pool")
                            else:
                                sg = stgB.get()
                                evac_copy(sg, sg[:, 0:ntok], pp, pp[:, 0:ntok])
                                if kind == "k":
                                    S.dma(kT_s[b, i * 128:(i + 1) * 128, tok0:tok0 + ntok], sg[:, 0:ntok], reads=[sg], q="pool")
                                else:
                                    S.dma(qT_s[b, i * 128:(i + 1) * 128, src0:src0 + ntok], sg[:, 0:ntok], reads=[sg], q="pool")
                        for ti in range(ntok // 128):
                            pp = psf.get()
                            for kc in range(8):
                                S.op("pe", lambda e, pp=pp, kc=kc, xnT=xnT, ti=ti: e.matmul(
                                    pp[:, :], lhsT=xnT[:, kc, ti * 128:(ti + 1) * 128], rhs=winb[:, kc, O_V:O_V + 512],
                                    start=(kc == 0), stop=(kc == 7)), [winb, xnT], [pp])
                            sv = stgV.get()
                            evac_copy(sv, sv[:, :, 0:64], pp, pp[:, :].rearrange("p (h e) -> p h e", e=64))
                            S.dma(v_s[b, tok0 + ti * 128: tok0 + (ti + 1) * 128, :], sv[:].rearrange("p h e -> p (h e)"),
                                  reads=[sv], q="pool")
                            if is_ctx:
                                continue
                            for (col0, dst) in ((O_GNA, sgna_s), (O_GRW, sgrw_s)):
                                pp = psf.get()
                                for kc in range(8):
                                    S.op("pe", lambda e, pp=pp, kc=kc, xnT=xnT, ti=ti, col0=col0: e.matmul(
                                        pp[:, :], lhsT=xnT[:, kc, ti * 128:(ti + 1) * 128], rhs=winb[:, kc, col0:col0 + 512],
                                        start=(kc == 0), stop=(kc == 7)), [winb, xnT], [pp])
                                sg = stgG.get()
                                S.op("act", lambda e, sg=sg, pp=pp: e.activation(sg[:], pp[:, :], AF.Silu), [pp], [sg])
                                S.dma(dst[b, src0 + ti * 128: src0 + (ti + 1) * 128, :], sg[:], reads=[sg], q="pool")
                S.barrier()

        if "s3" in stages:
            with ExitStack() as st:
                kT = S.sb(st, [128, 4, TT], BF16, "kT")
                qT = S.sb(st, [128, 4, T], BF16, "qT")
                vctx = S.sb(st, [128, 2, 8, 65], BF16, "vctx")
                tb = S.sb(st, [128, 8, 14, 64], F32, "tb")
                S.dma(tb[:], tb_d[:, :, :, :], writes=[tb])
                vwin_r = SbRing(S, st, 3, [128, 4, 8, 65], BF16, "vwin")
                sg_r = SbRing(S, st, 3, [64, 512], F32, "sgq")
                sl_r = SbRing(S, st, 3, [128, 256], F32, "sl")
                pt_r = SbRing(S, st, 3, [128, 384], BF16, "ptile")
                rec_r = SbRing(S, st, 2, [64, 4, 1], F32, "rec")
                na_r = SbRing(S, st, 3, [64, 512], F32, "na")
                for b in range(NB):
                    for hp in range(4):
                        S.dma(kT[:, hp, :], kT_s[b, hp * 128:(hp + 1) * 128, :], writes=[kT])
                        S.dma(qT[:, hp, :], qT_s[b, hp * 128:(hp + 1) * 128, :], writes=[qT], q="pool")
                    S.dma(vctx[:].rearrange("p k h e -> p k (h e)"),
                          v_s[b, 0:256, :].rearrange("(k p) c -> p k c", p=128), writes=[vctx])
                    for i in range(64):
                        r0 = min(max(i - 4, 0), 56)
                        rho = r0 - i + 7
                        vwin = vwin_r.get(); sgq = sg_r.get(); na = na_r.get()
                        S.dma(vwin[:].rearrange("p k h e -> p k (h e)"),
                              v_s[b, L + r0 * 64: L + (r0 + 8) * 64, :].rearrange("(k p) c -> p k c", p=128), writes=[vwin])
                        S.dma(sgq[:], sgna_s[b, i * 64:(i + 1) * 64, :], writes=[sgq], q="pool")
                        for hh in range(2):
                            po = psf.get()
                            for h4 in range(4):
                                h = hh * 4 + h4
                                hp, hb = h // 2, (h % 2) * 64
                                ps_ = psf.get()
                                qap = qT[hb:hb + 64, hp, i * 64:(i + 1) * 64]
                                for bi in range(4):
                                    k0 = L + (r0 + 2 * bi) * 64
                                    S.op("pe", lambda e, ps_=ps_, bi=bi, k0=k0, hb=hb, hp=hp, qap=qap: e.matmul(
                                        ps_[:, bi * 64:(bi + 1) * 64], lhsT=kT[hb:hb + 64, hp, k0:k0 + 128], rhs=qap,
                                        start=True, stop=True), [kT, qT], [ps_])
                                for cb in range(2):
                                    S.op("pe", lambda e, ps_=ps_, cb=cb, hb=hb, hp=hp, qap=qap: e.matmul(
                                        ps_[:, 256 + cb * 64:256 + (cb + 1) * 64], lhsT=kT[hb:hb + 64, hp, cb * 128:(cb + 1) * 128],
                                        rhs=qap, start=True, stop=True), [kT, qT], [ps_])
                                sl = sl_r.get(); ptile = pt_r.get()
                                S.op("dve", lambda e, ps_=ps_, sl=sl, h=h, rho=rho: e.scalar_tensor_tensor(
                                    out=sl[:].rearrange("p (k q) -> p k q", q=64),
                                    in0=ps_[:, 0:256].rearrange("p (k q) -> p k q", q=64), scalar=0.125,
                                    in1=tb[:, h, rho:rho + 7:2, :], op0=ALU.mult, op1=ALU.add), [ps_, tb], [sl])
                                S.op("act", lambda e, sl=sl, ptile=ptile: e.activation(ptile[:, 0:256], sl[:], AF.Exp), [sl], [ptile])
                                S.op("act", lambda e, ps_=ps_, ptile=ptile: e.activation(
                                    ptile[:, 256:384], ps_[:, 256:384], AF.Exp, scale=0.125), [ps_], [ptile])
                                for blk in range(6):
                                    rhs = vwin[:, blk, h, :] if blk < 4 else vctx[:, blk - 4, h, :]
                                    S.op("pe", lambda e, po=po, h4=h4, blk=blk, rhs=rhs, ptile=ptile: e.matmul(
                                        po[0:64, h4 * 65:(h4 + 1) * 65], lhsT=ptile[:, blk * 64:(blk + 1) * 64], rhs=rhs,
                                        start=(blk == 0), stop=(blk == 5)), [ptile, vwin, vctx], [po])
                            rec = rec_r.get()
                            po3 = po[0:64, 0:260].rearrange("p (h e) -> p h e", e=65)
                            S.op("dve", lambda e, rec=rec, po3=po3: e.reciprocal(rec[:], po3[:, :, 64:65]), [po], [rec])
                            na3 = na[:, hh * 256:(hh + 1) * 256].rearrange("p (h e) -> p h e", e=64)
                            S.op("dve", lambda e, rec=rec, po3=po3, na3=na3: e.tensor_tensor(
                                na3, po3[:, :, 0:64], bc(rec[:], [64, 4, 64]), ALU.mult), [po, rec], [na])
                        S.op("pool", lambda e, na=na, sgq=sgq: e.tensor_tensor(na[:], na[:], sgq[:], ALU.mult), [na, sgq], [na])
                        S.dma(nag_s[b, i * 64:(i + 1) * 64, :], na[:], reads=[na], q="pool")
                    S.barrier()

        if "s2a" in stages:
            with ExitStack() as st:
                def ld(name, src, shape, dt=F32):
                    t = S.sb(st, shape, dt, name)
                    S.dma(t[:], src, writes=[t])
                    return t
                convT = ld("convT", convT_d[:, :, :], [128, 14, 3])
                dw0T = ld("dw0T", dw0T_d[:, :, :], [128, 2, 4])
                a0T = ld("a0T", a0T_d[:, :, :], [128, 2, 4])
                dw2 = ld("dw2", dw2_d[:, :], [128, 512])
                aw2 = ld("aw2", aw2_d[:, :], [128, 512])
                kkT = ld("kkT", kkT_d[:, :], [128, 4])
                kaT = ld("kaT", kaT_d[:, :], [128, 4])
                rkT = ld("rkT", rkT_d[:, :], [128, 4])
                m1 = ld("m1", m1_d[:, :, :, :], [128, 2, 2, 64])
                m2 = ld("m2", m2_d[:, :, :, :], [128, 2, 2, 64])
                m3 = ld("m3", m3_d[:, :, :], [128, 2, 64])
                id2 = ld("id2", id2_d[:, :], [128, 64])
                bones = ld("bones", bones_d[:, :], [128, 128])
                rmask = ld("rmask", rmask_d[:, :], [128, 512])
                omka = S.sb(st, [128, 4], F32, "omka")
                S.op("dve", lambda e: e.tensor_scalar(omka[:], kaT[:], -1.0, 1.0, ALU.mult, ALU.add), [kaT], [omka])

                pin = S.sb(st, [128, 14, SN + 2], F32, "pin")
                u = S.sb(st, [128, 14, SN], F32, "u")
                th = S.sb(st, [128, SN], F32, "th")
                sig = S.sb(st, [128, 2, 4, SN], F32, "sig")
                av = S.sb(st, [128, 2, 4, SN], F32, "av")
                kk = S.sb(st, [128, 4, SN], F32, "kk")
                tmpA = S.sb(st, [128, 4, SN], F32, "tmpA")
                tmpB = S.sb(st, [128, 4, SN], F32, "tmpB")
                rn = S.sb(st, [128, 4, SN], F32, "rn")
                kd = [S.sb(st, [128, 4, SN], F32, "kd%d" % d) for d in range(2)]
                bb = S.sb(st, [128, 4, SN], F32, "bb")
                Psg = S.sb(st, [128, 4, SN], F32, "Psg")
                Qm = S.sb(st, [128, 4, SN], F32, "Qm")
                Em = S.sb(st, [128, 4, SN], F32, "Em")
                Gi = S.sb(st, [128, 4, SN], F32, "Gi")
                ex = [S.sb(st, [128, 4, SN], F32, "ex%d" % i) for i in range(4)]
                wtot = [S.sb(st, [128, 4, SN // 64], F32, "wtot%d" % d) for d in range(2)]
                krt = [S.sb(st, [128, 4, SN // 64, 2, 64], BF16, "krt%d" % d) for d in range(2)]
                kh = [S.sb(st, [128, 4, SN], BF16, "kh%d" % d) for d in range(2)]
                bh = [S.sb(st, [128, 4, SN], BF16, "bh%d" % d) for d in range(2)]
                kp = [S.sb(st, [128, 4, SN], BF16, "kp%d" % d) for d in range(2)]
                nbp = [S.sb(st, [128, 4, SN], BF16, "nbp%d" % d) for d in range(2)]
                vb = S.sb(st, [128, 4, SN], BF16, "vb")
                bonT = S.sb(st, [128, 4, SN], F32, "bonT")
                bstg = SbRing(S, st, 2, [128, 512], F32, "bstg")
                vtok = S.sb(st, [128, SN // 64, 4, 64], BF16, "vtok")
                NCH = 4
                WK = [S.sb(st, [128, 4, 6, 64], BF16, "WK") for _ in range(NCH)]
                AR = [S.sb(st, [128, 4, 2, 64], BF16, "AR") for _ in range(NCH)]
                PQ = [[S.sb(st, [128, 4, 2, 64], BF16, "PQ") for _ in range(2)] for _ in range(NCH)]
                ZZ = [[S.sb(st, [128, 4, 64], BF16, "ZZ") for _ in range(2)] for _ in range(NCH)]
                GH = [S.sb(st, [128, 4, 2, 64], BF16, "GH") for _ in range(NCH)]
                dWt = [S.sb(st, [128, 4, 64], F32, "dW") for _ in range(NCH)]
                O1 = [S.sb(st, [128, 4, 2, 64], BF16, "O1") for _ in range(NCH)]
                O2 = [S.sb(st, [128, 4, 2, 64], F32, "O2") for _ in range(NCH)]

                def v4(buf, n):
                    return buf[:, :, 0:n]

                def chain(slot, b, d, c, cg):
                    wk, ar, gh, dw_, o1, o2 = WK[slot], AR[slot], GH[slot], dWt[slot], O1[slot], O2[slot]
                    cs = slice(c * 64, (c + 1) * 64)
                    heads = [(hp, h2 * 64) for hp in range(4) for h2 in range(2)]
                    pt = psb.get()
                    ptv = pt[:, 0:768].rearrange("p (a s e) -> p a s e", a=4, s=3)
                    for (hp, hb) in heads:
                        for si, srcb in enumerate((nbp[d], None, kp[d])):
                            if srcb is None:
                                in_ap = krt[d][hb:hb + 64, hp, c, 0, :]
                                rb = krt[d]
                            else:
                                in_ap = srcb[hb:hb + 64, hp, cs]
                                rb = srcb
                            S.op("pe", lambda e, in_ap=in_ap, hp=hp, hb=hb, si=si: e.transpose(
                                ptv[hb:hb + 64, hp, si, :], in_ap, identb[hb:hb + 64, hb:hb + 64]), [rb, identb], [pt])
                    S.op("act", lambda e: e.copy(wk[:, :, 1:6:2, :], ptv), [pt], [wk])
                    p1 = psf.get(); p2 = psf.get(); p3 = psf.get()
                    p1v = p1[:, :].rearrange("p (a s e) -> p a s e", a=4, s=2)
                    p2v = p2[:, :].rearrange("p (a s e) -> p a s e", a=4, s=2)
                    p3v = p3[:, 0:256].rearrange("p (a e) -> p a e", a=4)
                    for (hp, hb) in heads:
                        rhs = krt[d][hb:hb + 64, hp, c, :, :]
                        S.op("pe", lambda e, hp=hp, hb=hb, rhs=rhs: e.matmul(
                            p1v[hb:hb + 64, hp, :, :], lhsT=bh[d][hb:hb + 64, hp, cs], rhs=rhs, start=True, stop=True),
                            [bh[d], krt[d]], [p1])
                        S.op("pe", lambda e, hp=hp, hb=hb, rhs=rhs: e.matmul(
                            p2v[hb:hb + 64, hp, :, :], lhsT=kh[d][hb:hb + 64, hp, cs], rhs=rhs, start=True, stop=True),
                            [kh[d], krt[d]], [p2])
                        S.op("pe", lambda e, hp=hp, hb=hb: e.matmul(
                            p3v[hb:hb + 64, hp, :], lhsT=krt[d][hb:hb + 64, hp, c, 0, :], rhs=bh[d][hb:hb + 64, hp, cs],
                            start=True, stop=True), [bh[d], krt[d]], [p3])
                    S.op("dve", lambda e: e.tensor_tensor(
                        wk[:, :, 0:3:2, :], p1v, bc(m1[:, d:d + 1, :, :], [128, 4, 2, 64]), ALU.mult), [p1, m1], [wk])
                    S.op("dve", lambda e: e.tensor_tensor(
                        ar[:], p2v, bc(m2[:, d:d + 1, :, :], [128, 4, 2, 64]), ALU.mult), [p2, m2], [ar])
                    pq0 = PQ[slot][0]
                    S.op("dve", lambda e: e.tensor_tensor(
                        pq0[:, :, 1, :], p3v, bc(m3[:, d:d + 1, :], [128, 4, 64]), ALU.mult), [p3, m3], [pq0])
                    S.op("pool", lambda e: e.tensor_copy(pq0[:, :, 0, :], wk[:, :, 0, :]), [wk], [pq0])
                    z = ZZ[slot][0]
                    S.op("pool", lambda e: e.tensor_tensor(
                        z[:], bc(id2[:].rearrange("p (o e) -> p o e", o=1), [128, 4, 64]), wk[:, :, 0, :], ALU.subtract),
                        [id2, wk], [z])
                    yield
                    pq = pq0
                    for lev in range(1, 6):
                        pqn = PQ[slot][lev % 2]
                        pp = psf.get()
                        ppv = pp[:, :].rearrange("p (a s e) -> p a s e", a=4, s=2)
                        for (hp, hb) in heads:
                            if lev < 5:
                                S.op("pe", lambda e, hp=hp, hb=hb, pq=pq: e.matmul(
                                    ppv[hb:hb + 64, hp, 0, :], lhsT=pq[hb:hb + 64, hp, 1, :], rhs=pq[hb:hb + 64, hp, 0, :],
                                    start=True, stop=True), [pq], [pp])
                            S.op("pe", lambda e, hp=hp, hb=hb, pq=pq: e.matmul(
                                ppv[hb:hb + 64, hp, 1, :], lhsT=pq[hb:hb + 64, hp, 0, :], rhs=pq[hb:hb + 64, hp, 1, :],
                                start=True, stop=True), [pq], [pp])
                        if lev < 5:
                            S.op("act", lambda e, pqn=pqn, ppv=ppv: e.copy(pqn[:], ppv), [pp], [pqn])
                        else:
                            S.op("act", lambda e, pqn=pqn, ppv=ppv: e.copy(pqn[:, :, 1, :], ppv[:, :, 1, :]), [pp], [pqn])
                        yield
                        pz = psf.get()
                        pzv = pz[:, 0:256].rearrange("p (a e) -> p a e", a=4)
                        zn = ZZ[slot][lev % 2]
                        for (hp, hb) in heads:
                            S.op("pe", lambda e, hp=hp, hb=hb, pqn=pqn, z=z: e.matmul(
                                pzv[hb:hb + 64, hp, :], lhsT=pqn[hb:hb + 64, hp, 1, :], rhs=z[hb:hb + 64, hp, :],
                                start=True, stop=True), [pqn, z], [pz])
                        S.op("dve", lambda e, zn=zn, z=z, pzv=pzv: e.tensor_tensor(zn[:], pzv, z[:], ALU.add), [pz, z], [zn])
                        z = zn
                        pq = pqn
                        yield
                    pa = psf.get()
                    pav = pa[:, 0:256].rearrange("p (a e) -> p a e", a=4)
                    for (hp, hb) in heads:
                        S.op("pe", lambda e, hp=hp, hb=hb: e.matmul(
                            pav[hb:hb + 64, hp, :], lhsT=ar[hb:hb + 64, hp, 0, :], rhs=vtok[hb:hb + 64, c, hp, :],
                            start=True, stop=True), [ar, vtok], [pa])
                    S.op("act", lambda e: e.copy(wk[:, :, 4, :], pav), [pa], [wk])
                    yield
                    pg = psf.get()
                    pgv = pg[:, :].rearrange("p (a s e) -> p a s e", a=4, s=2)
                    for (hp, hb) in heads:
                        S.op("pe", lambda e, hp=hp, hb=hb, z=z: e.matmul(
                            pgv[hb:hb + 64, hp, :, :], lhsT=z[hb:hb + 64, hp, :], rhs=wk[hb:hb + 64, hp, 3:5, :],
                            start=True, stop=True), [z, wk], [pg])
                    S.op("act", lambda e: e.copy(gh[:], pgv), [pg], [gh])
                    yield
                    pph = psf.get()
                    pphv = pph[:, :].rearrange("p (a s e) -> p a s e", a=4, s=2)
                    pps = psf.get()
                    ppsv = pps[:, :].rearrange("p (a s e) -> p a s e", a=4, s=2)
                    for (hp, hb) in heads:
                        S.op("pe", lambda e, hp=hp, hb=hb: e.matmul(
                            pphv[hb:hb + 64, hp, :, :], lhsT=gh[hb:hb + 64, hp, 0, :], rhs=wk[hb:hb + 64, hp, 1:3, :],
                            start=True, stop=True), [gh, wk], [pph])
                    for (hp, hb) in heads:
                        vv = vtok[hb:hb + 64, c, hp, :]
                        hh_ = gh[hb:hb + 64, hp, 1, :]
                        S.op("pe", lambda e, hp=hp, hb=hb, vv=vv: e.matmul(
                            ppsv[hb:hb + 64, hp, 0, :], lhsT=wk[hb:hb + 64, hp, 5, :], rhs=vv, start=True, stop=False),
                            [wk, vtok], [pps])
                        S.op("pe", lambda e, hp=hp, hb=hb, hh_=hh_: e.matmul(
                            ppsv[hb:hb + 64, hp, 0, :], lhsT=wk[hb:hb + 64, hp, 1, :], rhs=hh_, start=False, stop=True),
                            [wk, gh], [pps])
                        S.op("pe", lambda e, hp=hp, hb=hb, vv=vv: e.matmul(
                            ppsv[hb:hb + 64, hp, 1, :], lhsT=ar[hb:hb + 64, hp, 1, :], rhs=vv, start=True, stop=False),
                            [ar, vtok], [pps])
                        S.op("pe", lambda e, hp=hp, hb=hb, hh_=hh_: e.matmul(
                            ppsv[hb:hb + 64, hp, 1, :], lhsT=wk[hb:hb + 64, hp, 2, :], rhs=hh_, start=False, stop=True),
                            [wk, gh], [pps])
                    S.op("pool", lambda e: e.tensor_tensor(
                        dw_[:], bc(id2[:].rearrange("p (o e) -> p o e", o=1), [128, 4, 64]),
                        bc(wtot[d][:, :, c:c + 1], [128, 4, 64]), ALU.mult), [id2, wtot[d]], [dw_])
                    S.op("dve", lambda e: e.tensor_tensor(o1[:, :, 0, :], pphv[:, :, 0, :], dw_[:], ALU.add), [pph, dw_], [o1])
                    S.op("dve", lambda e: e.tensor_tensor(o1[:, :, 1, :], pphv[:, :, 1, :], krt[d][:, :, c, 1, :], ALU.add),
                         [pph, krt[d]], [o1])
                    S.op("act", lambda e: e.copy(o2[:], ppsv), [pps], [o2])
                    S.dma(phiy_s[b, d, cg, :, :], o1[:].rearrange("p a s e -> p (a s e)"), reads=[o1], q="sp")
                    S.dma(psiy_s[b, d, cg, :, :], o2[:].rearrange("p a s e -> p (a s e)"), reads=[o2], q="pool")
                    yield

                for b in range(NB):
                    sts = [(True, SN, t0_, t0_ == 0, t0_ + SN == L) for t0_ in range(0, L, SN)]
                    sts += [(False, SN, t0_, t0_ == L, t0_ + SN == TT) for t0_ in range(L, TT, SN)]
                    for (is_ctx, n, tok0, left_zero, right_zero) in sts:
                        nchk = n // 64
                        nrw = 10 if is_ctx else 14
                        lo = tok0 - (0 if left_zero else 1)
                        hi = tok0 + n + (0 if right_zero else 1)
                        plo = 1 - (tok0 - lo)
                        if left_zero:
                            S.op("pool", lambda e: e.memset(pin[:, :, 0:1], 0.0), [], [pin])
                        if right_zero:
                            S.op("pool", lambda e, n=n: e.memset(pin[:, :, n + 1:n + 2], 0.0), [], [pin])
                        for ch in range(nrw):
                            S.dma(pin[:, ch, plo:plo + (hi - lo)], rwT_s[b, ch * 128:(ch + 1) * 128, lo:hi], writes=[pin],
                                  q=("sp" if ch % 2 == 0 else "pool"))
                        if is_ctx:
                            S.op("pool", lambda e: e.memset(u[:, 10:14, :], 0.0), [], [u])
                        for ch in range(nrw):
                            S.op("act", lambda e, ch=ch: e.activation(
                                u[:, ch, 0:n], pin[:, ch, 0:n], AF.Identity, scale=convT[:, ch, 0:1]), [pin, convT], [u])
                            S.op("dve", lambda e, ch=ch: e.scalar_tensor_tensor(
                                out=u[:, ch, 0:n], in0=pin[:, ch, 1:n + 1], scalar=convT[:, ch, 1:2], in1=u[:, ch, 0:n],
                                op0=ALU.mult, op1=ALU.add), [pin, convT, u], [u])
                            S.op("dve", lambda e, ch=ch: e.scalar_tensor_tensor(
                                out=u[:, ch, 0:n], in0=pin[:, ch, 2:n + 2], scalar=convT[:, ch, 2:3], in1=u[:, ch, 0:n],
                                op0=ALU.mult, op1=ALU.add), [pin, convT, u], [u])
                        S.op("act", lambda e: e.activation(th[:, 0:n], u[:, 8, 0:n], AF.Tanh), [u], [th])
                        for d in range(2):
                            ds_ = slice(d * 64, (d + 1) * 64)
                            for hp in range(4):
                                pp = psf.get()
                                S.op("pe", lambda e, pp=pp, hp=hp, ds_=ds_: e.matmul(
                                    pp[:, 0:n], lhsT=dw2[ds_, hp * 128:(hp + 1) * 128], rhs=th[ds_, 0:n], start=True, stop=True),
                                    [dw2, th], [pp])
                                S.op("act", lambda e, pp=pp, hp=hp, d=d: e.activation(
                                    sig[:, d, hp, 0:n], pp[:, 0:n], AF.Sigmoid, bias=dw0T[:, d, hp:hp + 1]), [pp, dw0T], [sig])
                                pp2 = psf.get()
                                S.op("pe", lambda e, pp2=pp2, hp=hp, ds_=ds_: e.matmul(
                                    pp2[:, 0:n], lhsT=aw2[ds_, hp * 128:(hp + 1) * 128], rhs=u[ds_, 9, 0:n], start=True, stop=True),
                                    [aw2, u], [pp2])
                                S.op("act", lambda e, pp2=pp2, hp=hp, d=d: e.activation(
                                    av[:, d, hp, 0:n], pp2[:, 0:n], AF.Sigmoid, bias=a0T[:, d, hp:hp + 1]), [pp2, a0T], [av])
                        uk = u[:, 0:4, 0:n]; uv = u[:, 4:8, 0:n]; ur = u[:, 10:14, 0:n]
                        S.op("pool", lambda e: e.tensor_tensor(
                            v4(kk, n), uk, bc(kkT[:].rearrange("p (a o) -> p a o", o=1), [128, 4, n]), ALU.mult), [u, kkT], [kk])
                        S.op("pool", lambda e: e.tensor_tensor(v4(tmpA, n), v4(kk, n), v4(kk, n), ALU.mult), [kk], [tmpA])
                        S.op("pool", lambda e: e.tensor_copy(v4(vb, n), uv), [u], [vb])
                        for hp in range(4):
                            pp = psf.get()
                            S.op("pe", lambda e, pp=pp, hp=hp: e.matmul(
                                pp[:, 0:n], lhsT=bones[:], rhs=tmpA[:, hp, 0:n], start=True, stop=True), [bones, tmpA], [pp])
                            S.op("dve", lambda e, pp=pp, hp=hp: e.tensor_scalar(
                                rn[:, hp, 0:n], pp[:, 0:n], 1e-24, None, ALU.max), [pp], [rn])
                        S.op("act", lambda e: e.activation(v4(rn, n), v4(rn, n), AF.Sqrt), [rn], [rn])
                        S.op("dve", lambda e: e.reciprocal(v4(rn, n), v4(rn, n)), [rn], [rn])
                        S.op("pool", lambda e: e.tensor_tensor(v4(kk, n), v4(kk, n), v4(rn, n), ALU.mult), [kk, rn], [kk])
                        for d in range(2):
                            S.op("pool", lambda e, d=d: e.tensor_tensor(
                                v4(tmpA, n), av[:, d, :, 0:n], bc(kaT[:].rearrange("p (a o) -> p a o", o=1), [128, 4, n]), ALU.mult),
                                [av, kaT], [tmpA])
                            S.op("pool", lambda e: e.tensor_tensor(
                                v4(tmpA, n), v4(tmpA, n), bc(omka[:].rearrange("p (a o) -> p a o", o=1), [128, 4, n]), ALU.add),
                                [tmpA, omka], [tmpA])
                            S.op("pool", lambda e, d=d: e.tensor_tensor(v4(kd[d], n), v4(tmpA, n), uk, ALU.mult), [tmpA, u], [kd[d]])
                        if not is_ctx:
                            pbs = [psf.get() for _ in range(4)]
                            for d in range(2):
                                S.op("pool", lambda e, d=d: e.tensor_tensor(v4(tmpB, n), v4(kd[d], n), ur, ALU.mult), [kd[d], u], [tmpB])
                                S.op("pool", lambda e: e.tensor_tensor(
                                    v4(tmpB, n), v4(tmpB, n), bc(rkT[:].rearrange("p (a o) -> p a o", o=1), [128, 4, n]), ALU.mult),
                                    [tmpB, rkT], [tmpB])
                                for hp in range(4):
                                    S.op("pe", lambda e, hp=hp, d=d: e.matmul(
                                        pbs[hp][:, 0:n], lhsT=bones[:], rhs=tmpB[:, hp, 0:n], start=(d == 0), stop=(d == 1)),
                                        [bones, tmpB], [pbs[hp]])
                                if d == 0:
                                    pass
                            for hp in range(4):
                                S.op("dve", lambda e, hp=hp: e.tensor_tensor(
                                    bonT[:, hp, 0:n], pbs[hp][:, 0:n], u[:, 4 + hp, 0:n], ALU.mult), [pbs[hp], u], [bonT])
                            for ti in range(n // 128):
                                pp = psf.get()
                                for hp in range(4):
                                    S.op("pe", lambda e, pp=pp, hp=hp, ti=ti: e.transpose(
                                        pp[:, hp * 128:(hp + 1) * 128], bonT[:, hp, ti * 128:(ti + 1) * 128], identf[:]),
                                        [bonT, identf], [pp])
                                bs = bstg.get()
                                S.op("act", lambda e, bs=bs, pp=pp: e.copy(bs[:], pp[:, :]), [pp], [bs])
                                t_lat = tok0 - L + ti * 128
                                S.dma(bonus_s[b, t_lat:t_lat + 128, :], bs[:], reads=[bs], q="pool")
                        for c in range(nchk):
                            pt = psb.get()
                            ptv = pt[:, 0:256].rearrange("p (a e) -> p a e", a=4)
                            for hp in range(4):
                                for hb in (0, 64):
                                    S.op("pe", lambda e, ptv=ptv, hp=hp, hb=hb, c=c, pt=pt: e.transpose(
                                        ptv[hb:hb + 64, hp, :], vb[hb:hb + 64, hp, c * 64:(c + 1) * 64], identb[hb:hb + 64, hb:hb + 64]),
                                        [vb, identb], [pt])
                            S.op("act", lambda e, ptv=ptv, c=c, pt=pt: e.copy(vtok[:, c, :, :], ptv), [pt], [vtok])
                        for d in range(2):
                            for hp in range(4):
                                S.op("dve", lambda e, hp=hp, d=d: e.tensor_tensor_scan(
                                    Psg[:, hp, 0:n], rmask[:, 0:n], sig[:, d, hp, 0:n], 0.0, ALU.mult, ALU.add),
                                    [rmask, sig], [Psg])
                            S.op("pool", lambda e, d=d: e.tensor_tensor(v4(bb, n), v4(kk, n), av[:, d, :, 0:n], ALU.mult), [kk, av], [bb])
                            P4 = Psg[:, :, 0:n].rearrange("p a (c t) -> p a c t", t=64)
                            tot_b = bc(P4[:, :, :, 63:64], [128, 4, nchk, 64])
                            S.op("pool", lambda e, P4=P4, tot_b=tot_b: e.tensor_tensor(
                                Qm[:, :, 0:n].rearrange("p a (c t) -> p a c t", t=64), tot_b, P4, ALU.subtract), [Psg], [Qm])
                            S.op("pool", lambda e, d=d: e.tensor_tensor(v4(Em, n), v4(Psg, n), sig[:, d, :, 0:n], ALU.subtract), [Psg, sig], [Em])
                            if d == 0:
                                gi, ge, tg = Psg, Em, Qm
                            else:
                                S.op("pool", lambda e, d=d: e.tensor_tensor(v4(Gi, n), v4(Qm, n), sig[:, d, :, 0:n], ALU.add), [Qm, sig], [Gi])
                                gi, ge, tg = Gi, Qm, Em
                            S.op("act", lambda e, gi=gi: e.activation(v4(ex[0], n), v4(gi, n), AF.Exp, scale=-C0), [gi], [ex[0]])
                            S.op("act", lambda e, ge=ge: e.activation(v4(ex[1], n), v4(ge, n), AF.Exp, scale=-C0), [ge], [ex[1]])
                            S.op("act", lambda e, gi=gi: e.activation(v4(ex[2], n), v4(gi, n), AF.Exp, scale=C0), [gi], [ex[2]])
                            S.op("act", lambda e, tg=tg: e.activation(v4(ex[3], n), v4(tg, n), AF.Exp, scale=-C0), [tg], [ex[3]])
                            S.op("act", lambda e, d=d, P4=P4: e.activation(
                                wtot[d][:, :, 0:nchk], P4[:, :, :, 63], AF.Exp, scale=-C0), [Psg], [wtot[d]])
                            k5 = krt[d][:, :, 0:nchk, :, :]
                            S.op("dve", lambda e, k5=k5: e.tensor_tensor(
                                k5[:, :, :, 0, :], kk[:, :, 0:n].rearrange("p a (c t) -> p a c t", t=64),
                                ex[1][:, :, 0:n].rearrange("p a (c t) -> p a c t", t=64), ALU.mult), [kk, ex[1]], [krt[d]])
                            S.op("pool", lambda e, k5=k5: e.tensor_tensor(
                                k5[:, :, :, 1, :], u[:, 10:14, 0:n].rearrange("p a (c t) -> p a c t", t=64),
                                ex[0][:, :, 0:n].rearrange("p a (c t) -> p a c t", t=64), ALU.mult), [u, ex[0]], [krt[d]])
                            S.op("dve", lambda e, d=d: e.tensor_tensor(v4(kh[d], n), v4(kd[d], n), v4(ex[2], n), ALU.mult), [kd[d], ex[2]], [kh[d]])
                            S.op("pool", lambda e, d=d: e.tensor_tensor(v4(bh[d], n), v4(bb, n), v4(ex[2], n), ALU.mult), [bb, ex[2]], [bh[d]])
                            S.op("dve", lambda e, d=d: e.tensor_tensor(v4(kp[d], n), v4(kd[d], n), v4(ex[3], n), ALU.mult), [kd[d], ex[3]], [kp[d]])
                            S.op("dve", lambda e, d=d: e.scalar_tensor_tensor(
                                out=v4(nbp[d], n), in0=v4(bb, n), scalar=-1.0, in1=v4(ex[3], n), op0=ALU.mult, op1=ALU.mult),
                                [bb, ex[3]], [nbp[d]])
                        for c0 in range(0, nchk, 2):
                            gens = []
                            for ci in range(2):
                                for d in range(2):
                                    c = c0 + ci
                                    gens.append(chain(ci * 2 + d, b, d, c, tok0 // 64 + c))
                            alive = list(gens)
                            while alive:
                                nxt = []
                                for g in alive:
                                    try:
                                        next(g)
                                        nxt.append(g)
                                    except StopIteration:
                                        pass
                                alive = nxt
                S.barrier()

        if "s2b" in stages:
            with ExitStack() as st:
                A_r = SbRing(S, st, 6, [128, 4, 2, 64], BF16, "Aphi")
                B_r = SbRing(S, st, 6, [128, 4, 2, 64], F32, "Bpsi")
                ST = [[S.sb(st, [128, 4, 64], BF16, "ST") for _ in range(2)] for _ in range(2)]
                yo_r = SbRing(S, st, 4, [128, 4, 64], F32, "yo")
                heads = [(hp, h2 * 64) for hp in range(4) for h2 in range(2)]
                order = [list(range(4)) + list(range(4, NCHUNK)),
                         list(range(3, -1, -1)) + list(range(NCHUNK - 1, 3, -1))]
                for b in range(NB):
                    for d in range(2):
                        S.op("pool", lambda e, d=d: e.memset(ST[d][0][:], 0.0), [], [ST[d][0]])
                    for n_ in range(NCHUNK):
                        for d in range(2):
                            cg = order[d][n_]
                            A = A_r.get(); Bm = B_r.get()
                            S.dma(A[:].rearrange("p a s e -> p (a s e)"), phiy_s[b, d, cg, :, :], writes=[A], q="sp")
                            S.dma(Bm[:].rearrange("p a s e -> p (a s e)"), psiy_s[b, d, cg, :, :], writes=[Bm], q="pool")
                            cur = ST[d][n_ % 2]; new = ST[d][(n_ + 1) % 2]
                            pS = psf.get()
                            pSv = pS[:, 0:512].rearrange("p (s a e) -> p s a e", s=2, a=4)
                            for (hp, hb) in heads:
                                S.op("pe", lambda e, hp=hp, hb=hb, A=A, cur=cur, pSv=pSv, pS=pS: e.matmul(
                                    pSv[hb:hb + 64, 0, hp, :], lhsT=A[hb:hb + 64, hp, 0, :], rhs=cur[hb:hb + 64, hp, :],
                                    start=True, stop=True), [A, cur], [pS])
                                if cg >= 4:
                                    S.op("pe", lambda e, hp=hp, hb=hb, A=A, cur=cur, pSv=pSv, pS=pS: e.matmul(
                                        pSv[hb:hb + 64, 1, hp, :], lhsT=A[hb:hb + 64, hp, 1, :], rhs=cur[hb:hb + 64, hp, :],
                                        start=True, stop=True), [A, cur], [pS])
                            S.op("dve", lambda e, new=new, pSv=pSv, Bm=Bm: e.tensor_tensor(
                                new[:], pSv[:, 0, :, :], Bm[:, :, 0, :], ALU.add), [pS, Bm], [new])
                            if cg >= 4:
                                yo = yo_r.get()
                                S.op("dve", lambda e, yo=yo, pSv=pSv, Bm=Bm: e.tensor_tensor(
                                    yo[:], pSv[:, 1, :, :], Bm[:, :, 1, :], ALU.add), [pS, Bm], [yo])
                                t0 = (cg - 4) * 64
                                dst = yd_s[b, d, t0:t0 + 64, :].rearrange("t (a h e) -> t a h e", a=4, h=2)
                                for h2 in range(2):
                                    S.dma(dst[:, :, h2, :], yo[h2 * 64:(h2 + 1) * 64, :, :], reads=[yo],
                                          q=("sp" if h2 == 0 else "pool"))
                S.barrier()

        if "s4" in stages:
            with ExitStack() as st:
                gnw = S.sb(st, [128, 512], F32, "gnw"); gnb = S.sb(st, [128, 512], F32, "gnb")
                fg = S.sb(st, [128, D], F32, "fg")
                S.dma(gnw[:], gnw_d[:, :], writes=[gnw]); S.dma(gnb[:], gnb_d[:, :], writes=[gnb])
                S.dma(fg[:], fg_d[:, :], writes=[fg])
                xt_r = SbRing(S, st, 2, [128, D], F32, "xt4")
                mix_r = SbRing(S, st, 2, [128, D], F32, "mix")
                y0_r = SbRing(S, st, 2, [128, 8, 64], F32, "y0")
                y1_r = SbRing(S, st, 2, [128, 8, 64], F32, "y1")
                bo_r = SbRing(S, st, 2, [128, 512], F32, "bo")
                sg_r = SbRing(S, st, 2, [128, 512], F32, "sg4")
                sq_r = SbRing(S, st, 2, [128, 8, 64], F32, "sq")
                st8_r = SbRing(S, st, 4, [128, 8, 1], F32, "st8")
                mixT_r = SbRing(S, st, 2, [128, 8, 128], BF16, "mixT")
                hn_r = SbRing(S, st, 2, [128, D], F32, "hn")
                junk = S.sb(st, [128, D], F32, "junk4")
                s1_r = SbRing(S, st, 4, [128, 1], F32, "s1")
                ob_r = SbRing(S, st, 2, [128, D], F32, "ob")
                woutf = S.sb(st, [128, 8, D], F32, "woutf")
                woutg1 = S.sb(st, [128, 8, D], BF16, "woutg")
                S.dma(woutf[:], wout_d.rearrange("(kc p) n -> p kc n", p=128), writes=[woutf], q="pool")
                for b in range(NB):
                    S.op("dve", lambda e, b=b: e.tensor_tensor(
                        woutg1[:], woutf[:], bc(gbc[b][:].rearrange("p (o n) -> p o n", o=1), [128, 8, D]), ALU.mult),
                        [woutf, gbc[b]], [woutg1])
                    for ti in range(T // 128):
                        ts_ = slice(ti * 128, (ti + 1) * 128)
                        xt = xt_r.get(); mix = mix_r.get(); y0 = y0_r.get(); y1 = y1_r.get(); bo = bo_r.get(); sg = sg_r.get()
                        S.dma(xt[:], x_d[b, ts_, :], writes=[xt])
                        S.dma(mix[:, 0:512], nag_s[b, ts_, :], writes=[mix], q="pool")
                        S.dma(y0[:].rearrange("p h e -> p (h e)"), yd_s[b, 0, ts_, :], writes=[y0])
                        S.dma(y1[:].rearrange("p h e -> p (h e)"), yd_s[b, 1, ts_, :], writes=[y1], q="pool")
                        S.dma(bo[:], bonus_s[b, ts_, :], writes=[bo])
                        S.dma(sg[:], sgrw_s[b, ts_, :], writes=[sg], q="pool")
                        S.op("pool", lambda e, y0=y0, y1=y1: e.tensor_tensor(y0[:], y0[:], y1[:], ALU.add), [y0, y1], [y0])
                        mu = st8_r.get(); var = st8_r.get(); sq = sq_r.get()
                        S.op("dve", lambda e, mu=mu, y0=y0: e.tensor_reduce(out=mu[:, :, 0], in_=y0[:], axis=AX.X, op=ALU.add), [y0], [mu])
                        S.op("dve", lambda e, mu=mu: e.tensor_scalar(mu[:], mu[:], -1.0 / 64, None, ALU.mult), [mu], [mu])
                        S.op("dve", lambda e, mu=mu, y0=y0: e.tensor_tensor(y0[:], y0[:], bc(mu[:], [128, 8, 64]), ALU.add), [y0, mu], [y0])
                        S.op("pool", lambda e, sq=sq, y0=y0: e.tensor_tensor(sq[:], y0[:], y0[:], ALU.mult), [y0], [sq])
                        S.op("dve", lambda e, var=var, sq=sq: e.tensor_reduce(out=var[:, :, 0], in_=sq[:], axis=AX.X, op=ALU.add), [sq], [var])
                        S.op("act", lambda e, var=var: e.activation(var[:], var[:], AF.Sqrt, bias=GN_EPS, scale=1.0 / 64), [var], [var])
                        S.op("dve", lambda e, var=var: e.reciprocal(var[:], var[:]), [var], [var])
                        S.op("dve", lambda e, var=var, y0=y0: e.tensor_tensor(y0[:], y0[:], bc(var[:], [128, 8, 64]), ALU.mult), [y0, var], [y0])
                        yf = y0[:].rearrange("p h e -> p (h e)")
                        S.op("pool", lambda e, yf=yf, y0=y0: e.tensor_tensor(yf, yf, gnw[:], ALU.mult), [y0, gnw], [y0])
                        S.op("pool", lambda e, yf=yf, y0=y0: e.tensor_tensor(yf, yf, gnb[:], ALU.add), [y0, gnb], [y0])
                        S.op("pool", lambda e, yf=yf, y0=y0, bo=bo: e.tensor_tensor(yf, yf, bo[:], ALU.add), [y0, bo], [y0])
                        S.op("pool", lambda e, yf=yf, y0=y0, sg=sg, mix=mix: e.tensor_tensor(mix[:, 512:1024], yf, sg[:], ALU.mult), [y0, sg], [mix])
                        mixT = mixT_r.get()
                        for half in range(2):
                            pt = psf.get()
                            for j in range(4):
                                fc = half * 4 + j
                                S.op("pe", lambda e, pt=pt, j=j, fc=fc, mix=mix: e.transpose(
                                    pt[:, j * 128:(j + 1) * 128], mix[:, fc * 128:(fc + 1) * 128], identf[:]), [mix, identf], [pt])
                            if half == 0:
                                S.op("act", lambda e, pt=pt, mixT=mixT: e.copy(
                                    mixT[:, 0:4, :], pt[:, :].rearrange("p (a t) -> p a t", a=4)), [pt], [mixT])
                            else:
                                S.op("dve", lambda e, pt=pt, mixT=mixT: e.tensor_copy(
                                    mixT[:, 4:8, :], pt[:, :].rearrange("p (a t) -> p a t", a=4)), [pt], [mixT])
                        hn = hn_r.get()
                        for nh in range(2):
                            pp = psf.get()
                            for fc in range(8):
                                S.op("pe", lambda e, pp=pp, fc=fc, nh=nh, mixT=mixT: e.matmul(
                                    pp[:, :], lhsT=mixT[:, fc, :], rhs=woutg1[:, fc, nh * 512:(nh + 1) * 512],
                                    start=(fc == 0), stop=(fc == 7)), [mixT, woutg1], [pp])
                            S.op("dve", lambda e, pp=pp, nh=nh, hn=hn, xt=xt: e.tensor_tensor(
                                hn[:, nh * 512:(nh + 1) * 512], pp[:, :], xt[:, nh * 512:(nh + 1) * 512], ALU.add), [pp, xt], [hn])
                        s1 = s1_r.get(); ob = ob_r.get()
                        S.op("act", lambda e, hn=hn, s1=s1: e.activation(junk[:], hn[:], AF.Square, accum_out=s1[:]), [hn], [junk, s1])
                        S.op("act", lambda e, s1=s1: e.activation(s1[:], s1[:], AF.Sqrt, bias=RMS_EPS, scale=1.0 / D), [s1], [s1])
                        S.op("dve", lambda e, s1=s1: e.reciprocal(s1[:], s1[:]), [s1], [s1])
                        S.op("dve", lambda e, hn=hn, s1=s1, ob=ob: e.scalar_tensor_tensor(
                            out=ob[:], in0=hn[:], scalar=s1[:, 0:1], in1=fg[:], op0=ALU.mult, op1=ALU.mult), [hn, s1, fg], [ob])
                        S.dma(out_d[b, ts_, :], ob[:], reads=[ob], q="sp")
                S.barrier()
        S.barrier()
    return nc


def _const_tables():
    idx = np.arange(64)
    mt_strict = [(idx[:, None] < idx[None, :]), (idx[:, None] > idx[None, :])]
    mt_le = [(idx[:, None] <= idx[None, :]), (idx[:, None] >= idx[None, :])]
    m_strict = [(idx[None, :] < idx[:, None]), (idx[None, :] > idx[:, None])]
    m1 = np.zeros((64, 2, 2, 64), np.float32)
    m2 = np.zeros((64, 2, 2, 64), np.float32)
    m3 = np.zeros((64, 2, 64), np.float32)
    for d in range(2):
        m1[:, d, 0, :] = mt_strict[d]
        m1[:, d, 1, :] = -mt_le[d].astype(np.float32)
        m2[:, d, 0, :] = mt_strict[d]
        m2[:, d, 1, :] = mt_le[d]
        m3[:, d, :] = m_strict[d]
    dup = lambda a: np.ascontiguousarray(np.concatenate([a, a], axis=0))
    id2 = dup(np.eye(64, dtype=np.float32))
    bones = np.zeros((128, 128), np.float32)
    bones[:64, :64] = 1.0
    bones[64:, 64:] = 1.0
    rmask = np.ones((128, 512), np.float32)
    rmask[:, 0::64] = 0.0
    identf = np.eye(128, dtype=np.float32)
    sel = np.zeros((3, NB, 128), np.float32)
    for b in range(NB):
        sel[b, b, :] = 1.0
    return dict(m1=dup(m1), m2=dup(m2), m3=dup(m3), id2=id2, bones=bones, rmask=rmask, identf=identf, sel=sel)


def _bias_table(rpb):
    c = np.arange(64)
    j = np.arange(64)
    c0 = np.clip(j - 8, 0, 48)
    inwin = (c[:, None] >= c0[None, :]) & (c[:, None] < c0[None, :] + 16)
    coff = np.clip(c[:, None] - j[None, :] + 15, 0, 30)
    tb = np.full((2, 64, 8, 14, 64), -30000.0, np.float32)
    for rr in range(2):
        for rho0 in range(14):
            g = rpb[:, rho0 + rr, :][:, coff]
            tb[rr, :, :, rho0, :] = np.where(inwin[:, None, :], np.transpose(g, (1, 0, 2)), np.float32(-30000.0))
    return np.ascontiguousarray(tb.reshape(128, 8, 14, 64))


def make_in_maps(inp):
    f = lambda a: np.ascontiguousarray(np.asarray(a, dtype=np.float32))
    x = f(inp["x"]); c = f(inp["c"]); ctx = f(inp["ctx"]); c_ctx = f(inp["c_ctx"])
    w_mod = f(inp["w_mod"])[0]; b_mod = f(inp["b_mod"])[0]; norm_g = f(inp["norm_g"])[0]
    w_in = f(inp["w_in"])[0]; conv_w = f(inp["conv_w"])[0]
    dw0 = f(inp["decay_w0"])[0]; dw2 = f(inp["decay_w2"])[0]; a0 = f(inp["aaa_a0"])[0]; a2 = f(inp["aaa_a2"])[0]
    k_k = f(inp["k_k"])[0]; k_a = f(inp["k_a"])[0]; r_k = f(inp["r_k"])[0].reshape(512)
    gn_w = f(inp["gn_w"])[0]; gn_b = f(inp["gn_b"])[0]; rpb = f(inp["na_rpb"])[0]
    w_out = f(inp["w_out"])[0]; final_g = f(inp["final_g"])
    consts = _const_tables()
    colT = lambda v: np.ascontiguousarray(v.reshape(-1, 128).T)
    shared = dict(
        w_mod=w_mod, bmodT=colT(b_mod), bgate=np.ascontiguousarray(np.broadcast_to(b_mod[2 * D:], (3, D))),
        normgT=colT(norm_g), w_in=w_in,
        convT=np.ascontiguousarray(conv_w.reshape(3, 14, 128).transpose(2, 1, 0)),
        dw0T=np.ascontiguousarray(dw0.reshape(2, 4, 128).transpose(2, 0, 1)),
        a0T=np.ascontiguousarray(a0.reshape(2, 4, 128).transpose(2, 0, 1)),
        dw2=np.ascontiguousarray(dw2.reshape(128, 512)), aw2=np.ascontiguousarray(a2.reshape(128, 512)),
        kkT=colT(k_k), kaT=colT(k_a), rkT=colT(r_k),
        gnw_bc=np.ascontiguousarray(np.broadcast_to(gn_w, (128, 512))),
        gnb_bc=np.ascontiguousarray(np.broadcast_to(gn_b, (128, 512))),
        fg_bc=np.ascontiguousarray(np.broadcast_to(final_g, (128, D))),
        w_out=w_out, tb=_bias_table(rpb), **consts)
    maps = []
    for core in range(8):
        b0 = core * NB
        cv = np.stack([c[b0], c[b0 + 1], c_ctx], axis=0)
        cT = np.ascontiguousarray(cv.reshape(3, 8, 128).transpose(2, 1, 0))
        m = dict(shared)
        m.update(x=np.ascontiguousarray(x[b0:b0 + NB]), ctx=np.ascontiguousarray(ctx[b0:b0 + NB]), cT=cT)
        maps.append(m)
    return maps


def kernel(**inputs):
    nc = build_program()
    maps = make_in_maps(inputs)
    res = run_bass_kernel_spmd(nc, maps, core_ids=list(range(8)))
    out = np.concatenate([np.asarray(r["out"], dtype=np.float32) for r in res.results], axis=0)
    return out
```

```python
import numpy as np
from contextlib import ExitStack
import ml_dtypes
import concourse.bass as bass
import concourse.mybir as mybir
from concourse.bass_utils import run_bass_kernel_spmd

F32 = mybir.dt.float32
BF16 = mybir.dt.bfloat16
AF = mybir.ActivationFunctionType
ALU = mybir.AluOpType
AX = mybir.AxisListType

D = 1024
T = 4096
L = 256
TT = T + L
DIN = 4352
NB = 2
C0 = float(np.exp(-0.5))
RMS_EPS = 1e-6
GN_EPS = 64e-5
NDS = 48
NCHUNK = TT // 64
SN = 128
O_V = 512
O_RW = 1024
O_Q = 2816
O_GNA = 3328
O_GRW = 3840

DEBUG = False
STAGES = ("s0", "s1", "s3", "s2a", "s2b", "s4")


class Buf:
    __slots__ = ("t", "w", "r", "name")

    def __init__(self, t=None, name=""):
        self.t = t
        self.w = None
        self.r = {}
        self.name = name

    def __getitem__(self, k):
        return self.t[k]


class Sched:
    def __init__(self, nc, stack):
        self.nc = nc
        self.E = {"pe": nc.tensor, "act": nc.scalar, "dve": nc.vector, "pool": nc.gpsimd, "sp": nc.sync}
        self.sem = {e: stack.enter_context(nc.semaphore("s_" + e)) for e in self.E}
        self.cnt = {e: 0 for e in self.E}
        self.seen = {e: {} for e in self.E}
        self.dsem = [stack.enter_context(nc.semaphore("d%d" % i)) for i in range(NDS)]
        self.dcnt = [0] * NDS
        self.dnext = 0
        self.uid = 0
        self.dq = 0
        self.snap_dirty = {}
        self.snap_cache = {}

    def sb(self, stack, shape, dt, name):
        self.uid += 1
        nm = "%s_%d" % (name, self.uid)
        return Buf(stack.enter_context(self.nc.sbuf_tensor(nm, list(shape), dt)), nm)

    def ps(self, stack, shape, dt, name):
        self.uid += 1
        nm = "%s_%d" % (name, self.uid)
        return Buf(stack.enter_context(self.nc.psum_tensor(nm, list(shape), dt)), nm)

    def _wait(self, e, evs):
        best = {}
        for ev in evs:
            if ev is None:
                continue
            k = id(ev[0])
            if k not in best or best[k][1] < ev[1]:
                best[k] = ev
        seen = self.seen[e]
        for k, ev in best.items():
            sem, val = ev[0], ev[1]
            if seen.get(k, 0) < val:
                self.E[e].wait_ge(sem, val)
                seen[k] = val
                self.snap_dirty[e] = True
                clk = ev[2] if len(ev) > 2 else None
                if clk:
                    for k2, v2 in clk.items():
                        if seen.get(k2, 0) < v2:
                            seen[k2] = v2

    def _snap(self, e):
        if self.snap_dirty.get(e, True):
            self.snap_cache[e] = dict(self.seen[e])
            self.snap_dirty[e] = False
        return self.snap_cache[e]

    def _deps(self, e, reads, writes):
        deps = []
        own = id(self.sem[e])
        for b in reads:
            if b.w is not None and not (e == "pe" and id(b.w[0]) == own):
                deps.append(b.w)
        for b in writes:
            if b.w is not None and not (e == "pe" and id(b.w[0]) == own):
                deps.append(b.w)
            for k, ev in b.r.items():
                if k != own:
                    deps.append(ev)
        return deps

    def _commit(self, ev, reads, writes):
        k = id(ev[0])
        for b in reads:
            b.r[k] = ev
        for b in writes:
            b.w = ev
            b.r = {}

    def op(self, e, fn, reads=(), writes=()):
        self._wait(e, self._deps(e, reads, writes))
        ins = fn(self.E[e])
        self.cnt[e] += 1
        ins.then_inc(self.sem[e], 1)
        ev = (self.sem[e], self.cnt[e], self._snap(e))
        self._commit(ev, reads, writes)
        return ev

    def dma(self, out_ap, in_ap, reads=(), writes=(), q=None, **kw):
        if q is None:
            q = "sp"
        i = self.dnext
        self.dnext = (i + 1) % NDS
        deps = self._deps(q, reads, writes)
        if self.dcnt[i] > 0:
            deps.append((self.dsem[i], self.dcnt[i]))
        self._wait(q, deps)
        self.E[q].dma_start(out=out_ap, in_=in_ap, **kw).then_inc(self.dsem[i], 16)
        self.dcnt[i] += 16
        ev = (self.dsem[i], self.dcnt[i], self._snap(q))
        self._commit(ev, reads, writes)
        return ev

    def barrier(self):
        evs = [(self.sem[e], self.cnt[e]) for e in self.E if self.cnt[e] > 0]
        evs += [(self.dsem[i], self.dcnt[i]) for i in range(NDS) if self.dcnt[i] > 0]
        for e in ("sp", "act", "dve", "pool", "pe"):
            self._wait(e, evs)


class PsRing:
    def __init__(self, S, stack, n, shape, dt, name):
        self.bufs = [S.ps(stack, shape, dt, name) for _ in range(n)]
        self.i = 0

    def get(self):
        b = self.bufs[self.i]
        self.i = (self.i + 1) % len(self.bufs)
        return b


class SbRing(PsRing):
    def __init__(self, S, stack, n, shape, dt, name):
        self.bufs = [S.sb(stack, shape, dt, name) for _ in range(n)]
        self.i = 0


def bc(ap, shape):
    return ap.to_broadcast(list(shape))


def build_program(stages=STAGES, debug=False, dbg_names=()):
    nc = bass.Bass("TRN2", target_bir_lowering=False)

    def din(name, shape, dt=F32):
        return nc.dram_tensor(name, list(shape), dt, kind="ExternalInput").ap()

    def dscr(name, shape, dt=F32):
        kind = "ExternalOutput" if (debug and name in dbg_names) else "Internal"
        return nc.dram_tensor(name, list(shape), dt, kind=kind).ap()

    x_d = din("x", [NB, T, D])
    ctx_d = din("ctx", [NB, L, D])
    cT_d = din("cT", [128, 8, 3])
    wmod_d = din("w_mod", [D, 3 * D])
    bmodT_d = din("bmodT", [128, 24])
    bgate_d = din("bgate", [3, D])
    sel_d = din("sel", [3, NB, 128])
    normgT_d = din("normgT", [128, 8])
    win_d = din("w_in", [D, DIN])
    convT_d = din("convT", [128, 14, 3])
    dw0T_d = din("dw0T", [128, 2, 4])
    a0T_d = din("a0T", [128, 2, 4])
    dw2_d = din("dw2", [128, 512])
    aw2_d = din("aw2", [128, 512])
    kkT_d = din("kkT", [128, 4])
    kaT_d = din("kaT", [128, 4])
    rkT_d = din("rkT", [128, 4])
    gnw_d = din("gnw_bc", [128, 512])
    gnb_d = din("gnb_bc", [128, 512])
    fg_d = din("fg_bc", [128, D])
    wout_d = din("w_out", [D, D])
    tb_d = din("tb", [128, 8, 14, 64])
    m1e_d = din("m1e", [128, 2, 2, 128])
    m2e_d = din("m2e", [128, 2, 2, 128])
    m3e_d = din("m3e", [128, 2, 128])
    bmask_d = din("bmask", [128, 4])
    bones_d = din("bones", [128, 128])
    rmask_d = din("rmask", [128, 512])
    identf_d = din("identf", [128, 128])
    out_d = nc.dram_tensor("out", [NB, T, D], F32, kind="ExternalOutput").ap()

    qT_s = dscr("qT_s", [NB, 512, T], BF16)
    kT_s = dscr("kT_s", [NB, 512, TT], BF16)
    v_s = dscr("v_s", [NB, TT, 8 * 65], BF16)
    rwT_s = dscr("rwT_s", [NB, 1792, TT])
    sgna_s = dscr("sgna_s", [NB, T, 512])
    sgrw_s = dscr("sgrw_s", [NB, T, 512])
    nag_s = dscr("nag_s", [NB, T, 512])
    phiy_s = dscr("phiy_s", [NB, 2, NCHUNK, 128, 1024], BF16)
    psiy_s = dscr("psiy_s", [NB, 2, NCHUNK, 128, 512])
    yd_s = dscr("yd_s", [NB, 2, T, 512])
    bonus_s = dscr("bonus_s", [NB, T, 512])
    mod_s = dscr("mod_s", [128, 72]) if (debug and "mod_s" in dbg_names) else None

    with ExitStack() as top:
        S = Sched(nc, top)
        psf = PsRing(S, top, 6, [128, 512], F32, "psf")
        psb = PsRing(S, top, 2, [128, 1024], BF16, "psb")

        scale1 = S.sb(top, [128, 8, 3], F32, "scale1")
        shift = S.sb(top, [128, 8, 3], F32, "shift")
        gbc = [S.sb(top, [128, D], F32, "gbc") for _ in range(NB)]
        identf = S.sb(top, [128, 128], F32, "identf")
        identb = S.sb(top, [128, 128], BF16, "identb")
        S.dma(identf[:], identf_d[:, :], writes=[identf])
        S.op("dve", lambda e: e.tensor_copy(identb[:], identf[:]), [identf], [identb])

        if "s0" in stages:
            with ExitStack() as st:
                wm = S.sb(st, [128, 8, 3 * D], F32, "wm")
                for kc in range(8):
                    S.dma(wm[:, kc, :], wmod_d[kc * 128:(kc + 1) * 128, :], writes=[wm],
                          q=("sp" if kc % 2 == 0 else "pool"))
                cT = S.sb(st, [128, 8, 3], F32, "cT")
                sc = S.sb(st, [128, 8, 3], F32, "sc")
                bmodT = S.sb(st, [128, 24], F32, "bmodT")
                normgT = S.sb(st, [128, 8], F32, "normgT")
                bgate = S.sb(st, [3, D], F32, "bgate")
                sel = S.sb(st, [3, NB, 128], F32, "sel")
                modT = S.sb(st, [128, 24, 3], F32, "modT")
                grow = S.sb(st, [3, D], F32, "grow")
                S.dma(cT[:], cT_d[:, :, :], writes=[cT])
                S.dma(bmodT[:], bmodT_d[:, :], writes=[bmodT])
                S.dma(normgT[:], normgT_d[:, :], writes=[normgT])
                S.dma(bgate[:], bgate_d[:, :], writes=[bgate])
                S.dma(sel[:], sel_d[:, :, :], writes=[sel])
                S.op("act", lambda e: e.activation(sc[:], cT[:], AF.Silu), [cT], [sc])
                pm = psf.get()
                for cc in range(24):
                    for kc in range(8):
                        S.op("pe", lambda e, cc=cc, kc=kc: e.matmul(
                            pm[:, cc * 3:(cc + 1) * 3], lhsT=wm[:, kc, cc * 128:(cc + 1) * 128], rhs=sc[:, kc, :],
                            start=(kc == 0), stop=(kc == 7)), [wm, sc], [pm])
                S.op("dve", lambda e: e.tensor_tensor(
                    modT[:], pm[:, 0:72].rearrange("p (c v) -> p c v", v=3),
                    bc(bmodT[:].rearrange("p (c o) -> p c o", o=1), [128, 24, 3]), ALU.add), [pm, bmodT], [modT])
                S.op("dve", lambda e: e.scalar_tensor_tensor(
                    out=scale1[:], in0=modT[:, 8:16, :], scalar=1.0,
                    in1=bc(normgT[:].rearrange("p (c o) -> p c o", o=1), [128, 8, 3]),
                    op0=ALU.add, op1=ALU.mult), [modT, normgT], [scale1])
                S.op("dve", lambda e: e.tensor_copy(shift[:], modT[:, 0:8, :]), [modT], [shift])
                if mod_s is not None:
                    S.dma(mod_s[:, :], modT[:].rearrange("p c v -> p (c v)"), reads=[modT])
                for nh in range(2):
                    pg = psf.get()
                    for kc in range(8):
                        S.op("pe", lambda e, nh=nh, kc=kc, pg=pg: e.matmul(
                            pg[0:3, :], lhsT=sc[:, kc, :], rhs=wm[:, kc, 2 * D + nh * 512:2 * D + (nh + 1) * 512],
                            start=(kc == 0), stop=(kc == 7)), [wm, sc], [pg])
                    S.op("dve", lambda e, nh=nh, pg=pg: e.tensor_tensor(
                        grow[:, nh * 512:(nh + 1) * 512], pg[0:3, :], bgate[:, nh * 512:(nh + 1) * 512], ALU.add),
                        [pg, bgate], [grow])
                for b in range(NB):
                    for nh in range(2):
                        pg = psf.get()
                        S.op("pe", lambda e, b=b, nh=nh, pg=pg: e.matmul(
                            pg[:, :], lhsT=sel[:, b, :], rhs=grow[:, nh * 512:(nh + 1) * 512], start=True, stop=True),
                            [sel, grow], [pg])
                        S.op("act", lambda e, b=b, nh=nh, pg=pg: e.copy(gbc[b][:, nh * 512:(nh + 1) * 512], pg[:, :]), [pg], [gbc[b]])
                S.barrier()

        if "s1" in stages:
            with ExitStack() as st:
                winb = S.sb(st, [128, 8, DIN], BF16, "winb")
                wstg = SbRing(S, st, 2, [128, 1088], F32, "wstg")
                eng_alt = 0
                for kc in range(8):
                    for cq in range(4):
                        ws = wstg.get()
                        S.dma(ws[:], win_d[kc * 128:(kc + 1) * 128, cq * 1088:(cq + 1) * 1088], writes=[ws],
                              q=("sp" if (kc * 4 + cq) % 2 == 0 else "pool"))
                        e_ = "dve" if eng_alt % 2 == 0 else "act"
                        eng_alt += 1
                        if e_ == "dve":
                            S.op("dve", lambda e, kc=kc, cq=cq, ws=ws: e.tensor_copy(
                                winb[:, kc, cq * 1088:(cq + 1) * 1088], ws[:]), [ws], [winb])
                        else:
                            S.op("act", lambda e, kc=kc, cq=cq, ws=ws: e.copy(
                                winb[:, kc, cq * 1088:(cq + 1) * 1088], ws[:]), [ws], [winb])
                xt_r = SbRing(S, st, 2, [128, D], F32, "xt")
                xs_r = SbRing(S, st, 2, [128, D], F32, "xs")
                junk = S.sb(st, [128, D], F32, "junk")
                ss_r = SbRing(S, st, 2, [128, 1], F32, "ss")
                rs_r = SbRing(S, st, 2, [128, 1], F32, "rs")
                xnT_r = SbRing(S, st, 2, [128, 8, 512], BF16, "xnT")
                stgF = SbRing(S, st, 3, [128, 512], F32, "stgF")
                stgB = SbRing(S, st, 3, [128, 512], BF16, "stgB")
                stgV = SbRing(S, st, 2, [128, 8, 65], BF16, "stgV")
                stgG = SbRing(S, st, 3, [128, 512], F32, "stgG")
                for sv in stgV.bufs:
                    S.op("pool", lambda e, sv=sv: e.memset(sv[:], 1.0), [], [sv])
                ev_alt = [0]

                def evac_copy(dst_buf, dst_ap, src_buf, src_ap):
                    ev_alt[0] += 1
                    if ev_alt[0] % 2 == 0:
                        S.op("dve", lambda e: e.tensor_copy(dst_ap, src_ap), [src_buf], [dst_buf])
                    else:
                        S.op("act", lambda e: e.copy(dst_ap, src_ap), [src_buf], [dst_buf])

                for b in range(NB):
                    sts = [(True, 0, 256, 0)] + [(False, i * 512, 512, L + i * 512) for i in range(8)]
                    for (is_ctx, src0, ntok, tok0) in sts:
                        v_idx = 2 if is_ctx else b
                        src = ctx_d if is_ctx else x_d
                        xnT = xnT_r.get()
                        for ti in range(ntok // 128):
                            xt = xt_r.get(); xs = xs_r.get(); ss = ss_r.get(); rs = rs_r.get()
                            S.dma(xt[:], src[b, src0 + ti * 128: src0 + (ti + 1) * 128, :], writes=[xt])
                            S.op("act", lambda e, xt=xt, ss=ss: e.activation(junk[:], xt[:], AF.Square, accum_out=ss[:]),
                                 [xt], [junk, ss])
                            S.op("act", lambda e, ss=ss, rs=rs: e.activation(rs[:], ss[:], AF.Sqrt, bias=RMS_EPS, scale=1.0 / D),
                                 [ss], [rs])
                            S.op("dve", lambda e, rs=rs: e.reciprocal(rs[:], rs[:]), [rs], [rs])
                            S.op("dve", lambda e, xt=xt, xs=xs, rs=rs: e.tensor_scalar(
                                xs[:], xt[:], rs[:, 0:1], None, ALU.mult), [xt, rs], [xs])
                            for half in range(2):
                                pt = psf.get()
                                for j in range(4):
                                    dc = half * 4 + j
                                    S.op("pe", lambda e, pt=pt, j=j, dc=dc, xs=xs: e.transpose(
                                        pt[:, j * 128:(j + 1) * 128], xs[:, dc * 128:(dc + 1) * 128], identf[:]),
                                        [xs, identf], [pt])
                                for j in range(4):
                                    dc = half * 4 + j
                                    if j % 2 == 0:
                                        S.op("act", lambda e, pt=pt, j=j, dc=dc, xnT=xnT, ti=ti: e.activation(
                                            xnT[:, dc, ti * 128:(ti + 1) * 128], pt[:, j * 128:(j + 1) * 128], AF.Identity,
                                            bias=shift[:, dc, v_idx:v_idx + 1], scale=scale1[:, dc, v_idx:v_idx + 1]),
                                            [pt, shift, scale1], [xnT])
                                    else:
                                        S.op("dve", lambda e, pt=pt, j=j, dc=dc, xnT=xnT, ti=ti: e.tensor_scalar(
                                            xnT[:, dc, ti * 128:(ti + 1) * 128], pt[:, j * 128:(j + 1) * 128],
                                            scale1[:, dc, v_idx:v_idx + 1], shift[:, dc, v_idx:v_idx + 1], ALU.mult, ALU.add),
                                            [pt, shift, scale1], [xnT])
                        fm = [("k", i, i * 128) for i in range(4)]
                        fm += [("rw", i, O_RW + i * 128) for i in range(10 if is_ctx else 14)]
                        if not is_ctx:
                            fm += [("q", i, O_Q + i * 128) for i in range(4)]
                        for (kind, i, col0) in fm:
                            pp = psf.get()
                            for kc in range(8):
                                S.op("pe", lambda e, pp=pp, kc=kc, col0=col0, xnT=xnT: e.matmul(
                                    pp[:, 0:ntok], lhsT=winb[:, kc, col0:col0 + 128], rhs=xnT[:, kc, 0:ntok],
                                    start=(kc == 0), stop=(kc == 7)), [winb, xnT], [pp])
                            if kind == "rw":
                                sg = stgF.get()
                                evac_copy(sg, sg[:, 0:ntok], pp, pp[:, 0:ntok])
                                S.dma(rwT_s[b, i * 128:(i + 1) * 128, tok0:tok0 + ntok], sg[:, 0:ntok], reads=[sg], q="pool")
                            else:
                                sg = stgB.get()
                                evac_copy(sg, sg[:, 0:ntok], pp, pp[:, 0:ntok])
                                if kind == "k":
                                    S.dma(kT_s[b, i * 128:(i + 1) * 128, tok0:tok0 + ntok], sg[:, 0:ntok], reads=[sg], q="pool")
                                else:
                                    S.dma(qT_s[b, i * 128:(i + 1) * 128, src0:src0 + ntok], sg[:, 0:ntok], reads=[sg], q="pool")
                        for ti in range(ntok // 128):
                            pp = psf.get()
                            for kc in range(8):
                                S.op("pe", lambda e, pp=pp, kc=kc, xnT=xnT, ti=ti: e.matmul(
                                    pp[:, :], lhsT=xnT[:, kc, ti * 128:(ti + 1) * 128], rhs=winb[:, kc, O_V:O_V + 512],
                                    start=(kc == 0), stop=(kc == 7)), [winb, xnT], [pp])
                            sv = stgV.get()
                            evac_copy(sv, sv[:, :, 0:64], pp, pp[:, :].rearrange("p (h e) -> p h e", e=64))
                            S.dma(v_s[b, tok0 + ti * 128: tok0 + (ti + 1) * 128, :], sv[:].rearrange("p h e -> p (h e)"),
                                  reads=[sv], q="pool")
                            if is_ctx:
                                continue
                            for (col0, dst) in ((O_GNA, sgna_s), (O_GRW, sgrw_s)):
                                pp = psf.get()
                                for kc in range(8):
                                    S.op("pe", lambda e, pp=pp, kc=kc, xnT=xnT, ti=ti, col0=col0: e.matmul(
                                        pp[:, :], lhsT=xnT[:, kc, ti * 128:(ti + 1) * 128], rhs=winb[:, kc, col0:col0 + 512],
                                        start=(kc == 0), stop=(kc == 7)), [winb, xnT], [pp])
                                sg = stgG.get()
                                S.op("act", lambda e, sg=sg, pp=pp: e.activation(sg[:], pp[:, :], AF.Silu), [pp], [sg])
                                S.dma(dst[b, src0 + ti * 128: src0 + (ti + 1) * 128, :], sg[:], reads=[sg], q="pool")
                S.barrier()

        if "s3" in stages:
            with ExitStack() as st:
                kT = S.sb(st, [128, 4, TT], BF16, "kT")
                qT = S.sb(st, [128, 4, T], BF16, "qT")
                vctx = S.sb(st, [128, 2, 8, 65], BF16, "vctx")
                tb = S.sb(st, [128, 8, 14, 64], F32, "tb")
                S.dma(tb[:], tb_d[:, :, :, :], writes=[tb])
                vwin_r = SbRing(S, st, 3, [128, 4, 8, 65], BF16, "vwin")
                sg_r = SbRing(S, st, 3, [64, 512], F32, "sgq")
                sl_r = SbRing(S, st, 3, [128, 256], F32, "sl")
                pt_r = SbRing(S, st, 3, [128, 384], BF16, "ptile")
                rec_r = SbRing(S, st, 2, [64, 4, 1], F32, "rec")
                na_r = SbRing(S, st, 3, [64, 512], F32, "na")
                for b in range(NB):
                    for hp in range(4):
                        S.dma(kT[:, hp, :], kT_s[b, hp * 128:(hp + 1) * 128, :], writes=[kT])
                        S.dma(qT[:, hp, :], qT_s[b, hp * 128:(hp + 1) * 128, :], writes=[qT], q="pool")
                    S.dma(vctx[:].rearrange("p k h e -> p k (h e)"),
                          v_s[b, 0:256, :].rearrange("(k p) c -> p k c", p=128), writes=[vctx])
                    for i in range(64):
                        r0 = min(max(i - 4, 0), 56)
                        rho = r0 - i + 7
                        vwin = vwin_r.get(); sgq = sg_r.get(); na = na_r.get()
                        S.dma(vwin[:].rearrange("p k h e -> p k (h e)"),
                              v_s[b, L + r0 * 64: L + (r0 + 8) * 64, :].rearrange("(k p) c -> p k c", p=128), writes=[vwin])
                        S.dma(sgq[:], sgna_s[b, i * 64:(i + 1) * 64, :], writes=[sgq], q="pool")
                        for hh in range(2):
                            po = psf.get()
                            for h4 in range(4):
                                h = hh * 4 + h4
                                hp, hb = h // 2, (h % 2) * 64
                                ps_ = psf.get()
                                qap = qT[hb:hb + 64, hp, i * 64:(i + 1) * 64]
                                for bi in range(4):
                                    k0 = L + (r0 + 2 * bi) * 64
                                    S.op("pe", lambda e, ps_=ps_, bi=bi, k0=k0, hb=hb, hp=hp, qap=qap: e.matmul(
                                        ps_[:, bi * 64:(bi + 1) * 64], lhsT=kT[hb:hb + 64, hp, k0:k0 + 128], rhs=qap,
                                        start=True, stop=True), [kT, qT], [ps_])
                                for cb in range(2):
                                    S.op("pe", lambda e, ps_=ps_, cb=cb, hb=hb, hp=hp, qap=qap: e.matmul(
                                        ps_[:, 256 + cb * 64:256 + (cb + 1) * 64], lhsT=kT[hb:hb + 64, hp, cb * 128:(cb + 1) * 128],
                                        rhs=qap, start=True, stop=True), [kT, qT], [ps_])
                                sl = sl_r.get(); ptile = pt_r.get()
                                S.op("dve", lambda e, ps_=ps_, sl=sl, h=h, rho=rho: e.scalar_tensor_tensor(
                                    out=sl[:].rearrange("p (k q) -> p k q", q=64),
                                    in0=ps_[:, 0:256].rearrange("p (k q) -> p k q", q=64), scalar=0.125,
                                    in1=tb[:, h, rho:rho + 7:2, :], op0=ALU.mult, op1=ALU.add), [ps_, tb], [sl])
                                S.op("act", lambda e, sl=sl, ptile=ptile: e.activation(ptile[:, 0:256], sl[:], AF.Exp), [sl], [ptile])
                                S.op("act", lambda e, ps_=ps_, ptile=ptile: e.activation(
                                    ptile[:, 256:384], ps_[:, 256:384], AF.Exp, scale=0.125), [ps_], [ptile])
                                for blk in range(6):
                                    rhs = vwin[:, blk, h, :] if blk < 4 else vctx[:, blk - 4, h, :]
                                    S.op("pe", lambda e, po=po, h4=h4, blk=blk, rhs=rhs, ptile=ptile: e.matmul(
                                        po[0:64, h4 * 65:(h4 + 1) * 65], lhsT=ptile[:, blk * 64:(blk + 1) * 64], rhs=rhs,
                                        start=(blk == 0), stop=(blk == 5)), [ptile, vwin, vctx], [po])
                            rec = rec_r.get()
                            po3 = po[0:64, 0:260].rearrange("p (h e) -> p h e", e=65)
                            S.op("dve", lambda e, rec=rec, po3=po3: e.reciprocal(rec[:], po3[:, :, 64:65]), [po], [rec])
                            na3 = na[:, hh * 256:(hh + 1) * 256].rearrange("p (h e) -> p h e", e=64)
                            S.op("dve", lambda e, rec=rec, po3=po3, na3=na3: e.tensor_tensor(
                                na3, po3[:, :, 0:64], bc(rec[:], [64, 4, 64]), ALU.mult), [po, rec], [na])
                        S.op("pool", lambda e, na=na, sgq=sgq: e.tensor_tensor(na[:], na[:], sgq[:], ALU.mult), [na, sgq], [na])
                        S.dma(nag_s[b, i * 64:(i + 1) * 64, :], na[:], reads=[na], q="pool")
                    S.barrier()

        if "s2a" in stages:
            with ExitStack() as st:
                def ld(name, src, shape, dt=F32):
                    t = S.sb(st, shape, dt, name)
                    S.dma(t[:], src, writes=[t])
                    return t
                convT = ld("convT", convT_d[:, :, :], [128, 14, 3])
                dw0T = ld("dw0T", dw0T_d[:, :, :], [128, 2, 4])
                a0T = ld("a0T", a0T_d[:, :, :], [128, 2, 4])
                dw2 = ld("dw2", dw2_d[:, :], [128, 512])
                aw2 = ld("aw2", aw2_d[:, :], [128, 512])
                kkT = ld("kkT", kkT_d[:, :], [128, 4])
                kaT = ld("kaT", kaT_d[:, :], [128, 4])
                rkT = ld("rkT", rkT_d[:, :], [128, 4])
                m1e = ld("m1e", m1e_d[:, :, :, :], [128, 2, 2, 128])
                m2e = ld("m2e", m2e_d[:, :, :, :], [128, 2, 2, 128])
                m3e = ld("m3e", m3e_d[:, :, :], [128, 2, 128])
                bmask = ld("bmask", bmask_d[:, :], [128, 4])
                bones = ld("bones", bones_d[:, :], [128, 128])
                rmask = ld("rmask", rmask_d[:, :], [128, 512])
                omka = S.sb(st, [128, 4], F32, "omka")
                S.op("dve", lambda e: e.tensor_scalar(omka[:], kaT[:], -1.0, 1.0, ALU.mult, ALU.add), [kaT], [omka])
                NCK = SN // 64
                pins = [dict(k=S.sb(st, [128, 4, SN + 2], F32, "pin_k"), v=S.sb(st, [128, 4, SN + 2], F32, "pin_v"),
                             r=S.sb(st, [128, 4, SN + 2], F32, "pin_r"), w=S.sb(st, [128, SN + 2], F32, "pin_w"),
                             a=S.sb(st, [128, SN + 2], F32, "pin_a")) for _ in range(2)]
                u_k = S.sb(st, [128, 4, SN], F32, "u_k"); u_v = S.sb(st, [128, 4, SN], F32, "u_v")
                u_r = S.sb(st, [128, 4, SN], F32, "u_r"); u_w = S.sb(st, [128, SN], F32, "u_w"); u_a = S.sb(st, [128, SN], F32, "u_a")
                ct = [S.sb(st, [128, 4, SN], F32, "ct%d" % i) for i in range(2)]
                th = S.sb(st, [128, SN], F32, "th")
                sig = [S.sb(st, [128, 4, SN], F32, "sig%d" % d) for d in range(2)]
                av = [S.sb(st, [128, 4, SN], F32, "av%d" % d) for d in range(2)]
                kk = S.sb(st, [128, 4, SN], F32, "kk")
                kk0 = S.sb(st, [128, 4, SN], F32, "kk0")
                sq = S.sb(st, [128, 4, SN], F32, "sq")
                rn = S.sb(st, [128, 4, SN], F32, "rn")
                tA = [S.sb(st, [128, 4, SN], F32, "tA%d" % d) for d in range(2)]
                tB = [S.sb(st, [128, 4, SN], F32, "tB%d" % d) for d in range(2)]
                kd = [S.sb(st, [128, 4, SN], F32, "kd%d" % d) for d in range(2)]
                bb = [S.sb(st, [128, 4, SN], F32, "bb%d" % d) for d in range(2)]
                Psg = [S.sb(st, [128, 4, SN], F32, "Psg%d" % d) for d in range(2)]
                Qm = [S.sb(st, [128, 4, SN], F32, "Qm%d" % d) for d in range(2)]
                Em = [S.sb(st, [128, 4, SN], F32, "Em%d" % d) for d in range(2)]
                Gi = S.sb(st, [128, 4, SN], F32, "Gi")
                ex = [[S.sb(st, [128, 4, SN], F32, "ex%d_%d" % (d, i)) for i in range(4)] for d in range(2)]
                wtot = [S.sb(st, [128, 4, NCK], F32, "wtot%d" % d) for d in range(2)]
                krt = [S.sb(st, [128, 4, NCK, 2, 2, 64], BF16, "krt%d" % d) for d in range(2)]
                kh = [S.sb(st, [128, 4, NCK, 2, 64], BF16, "kh%d" % d) for d in range(2)]
                bh = [S.sb(st, [128, 4, NCK, 2, 64], BF16, "bh%d" % d) for d in range(2)]
                kp = [S.sb(st, [128, 4, NCK, 2, 64], BF16, "kp%d" % d) for d in range(2)]
                nbp = [S.sb(st, [128, 4, NCK, 2, 64], BF16, "nbp%d" % d) for d in range(2)]
                vb = S.sb(st, [128, 4, SN], BF16, "vb")
                bonT = S.sb(st, [128, 4, SN], F32, "bonT")
                bstg = SbRing(S, st, 2, [128, 512], F32, "bstg")
                vtok = S.sb(st, [128, NCK, 4, 64], BF16, "vtok")
                NCH = 4
                WK = [S.sb(st, [128, 2, 6, 128], BF16, "WK") for _ in range(NCH)]
                AR = [S.sb(st, [128, 2, 2, 128], BF16, "AR") for _ in range(NCH)]
                PQ = [[S.sb(st, [128, 2, 2, 128], BF16, "PQ") for _ in range(2)] for _ in range(NCH)]
                ZZ = [[S.sb(st, [128, 2, 128], BF16, "ZZ") for _ in range(2)] for _ in range(NCH)]
                GH = [S.sb(st, [128, 2, 192], BF16, "GH") for _ in range(NCH)]
                dWt = [S.sb(st, [128, 2, 128], F32, "dW") for _ in range(NCH)]
                O1 = [S.sb(st, [128, 2, 2, 128], BF16, "O1") for _ in range(NCH)]
                O2 = [S.sb(st, [128, 2, 2, 64], F32, "O2") for _ in range(NCH)]

                def chain(slot, b, d, c, cg, g2):
                    wk, ar, gh, dw_, o1, o2 = WK[slot], AR[slot], GH[slot], dWt[slot], O1[slot], O2[slot]
                    hps = [(0, 2 * g2), (1, 2 * g2 + 1)]
                    gs = slice(2 * g2, 2 * g2 + 2)
                    pt = psb.get()
                    ptv = pt[:, 0:768].rearrange("p (a s e) -> p a s e", a=2, s=3)
                    for (l, hp) in hps:
                        srcs = ((nbp[d], nbp[d][:, hp, c, :, :]), (krt[d], krt[d][:, hp, c, 0, :, :]), (kp[d], kp[d][:, hp, c, :, :]))
                        for si, (rb, in_ap) in enumerate(srcs):
                            S.op("pe", lambda e, in_ap=in_ap, l=l, si=si: e.transpose(
                                ptv[:, l, si, :], in_ap.rearrange("p g t -> p (g t)"), identb[:]), [rb, identb], [pt])
                    S.op("act", lambda e: e.copy(wk[:, :, 1:6:2, :], ptv), [pt], [wk])
                    p1 = psf.get(); p2 = psf.get(); p3 = psf.get()
                    p1v = p1[:, :].rearrange("p (a s e) -> p a s e", a=2, s=2)
                    p2v = p2[:, :].rearrange("p (a s e) -> p a s e", a=2, s=2)
                    p3v = p3[:, 0:256].rearrange("p (a e) -> p a e", a=2)
                    for (l, hp) in hps:
                        rhs = krt[d][:, hp, c, :, :, :].rearrange("p s g t -> p (s g t)")
                        bh_ = bh[d][:, hp, c, :, :].rearrange("p g t -> p (g t)")
                        kh_ = kh[d][:, hp, c, :, :].rearrange("p g t -> p (g t)")
                        kk_ = krt[d][:, hp, c, 0, :, :].rearrange("p g t -> p (g t)")
                        S.op("pe", lambda e, l=l, rhs=rhs, bh_=bh_: e.matmul(
                            p1v[:, l, :, :].rearrange("p s e -> p (s e)"), lhsT=bh_, rhs=rhs, start=True, stop=True),
                            [bh[d], krt[d]], [p1])
                        S.op("pe", lambda e, l=l, rhs=rhs, kh_=kh_: e.matmul(
                            p2v[:, l, :, :].rearrange("p s e -> p (s e)"), lhsT=kh_, rhs=rhs, start=True, stop=True),
                            [kh[d], krt[d]], [p2])
                        S.op("pe", lambda e, l=l, kk_=kk_, bh_=bh_: e.matmul(
                            p3v[:, l, :], lhsT=kk_, rhs=bh_, start=True, stop=True), [bh[d], krt[d]], [p3])
                    S.op("dve", lambda e: e.tensor_tensor(
                        wk[:, :, 0:3:2, :], p1v, bc(m1e[:, d:d + 1, :, :], [128, 2, 2, 128]), ALU.mult), [p1, m1e], [wk])
                    S.op("dve", lambda e: e.tensor_tensor(
                        ar[:], p2v, bc(m2e[:, d:d + 1, :, :], [128, 2, 2, 128]), ALU.mult), [p2, m2e], [ar])
                    pq0 = PQ[slot][0]
                    S.op("dve", lambda e: e.tensor_tensor(
                        pq0[:, :, 1, :], p3v, bc(m3e[:, d:d + 1, :], [128, 2, 128]), ALU.mult), [p3, m3e], [pq0])
                    S.op("pool", lambda e: e.tensor_copy(pq0[:, :, 0, :], wk[:, :, 0, :]), [wk], [pq0])
                    z = ZZ[slot][0]
                    S.op("pool", lambda e: e.tensor_tensor(
                        z[:], bc(identb[:].rearrange("p (o e) -> p o e", o=1), [128, 2, 128]), wk[:, :, 0, :], ALU.subtract),
                        [identb, wk], [z])
                    yield
                    pq = pq0
                    for lev in range(1, 6):
                        pqn = PQ[slot][lev % 2]
                        pp = psf.get()
                        ppv = pp[:, :].rearrange("p (a s e) -> p a s e", a=2, s=2)
                        for (l, hp) in hps:
                            if lev < 5:
                                S.op("pe", lambda e, l=l, pq=pq: e.matmul(
                                    ppv[:, l, 0, :], lhsT=pq[:, l, 1, :], rhs=pq[:, l, 0, :], start=True, stop=True), [pq], [pp])
                            S.op("pe", lambda e, l=l, pq=pq: e.matmul(
                                ppv[:, l, 1, :], lhsT=pq[:, l, 0, :], rhs=pq[:, l, 1, :], start=True, stop=True), [pq], [pp])
                        if lev < 5:
                            S.op("act", lambda e, pqn=pqn, ppv=ppv: e.copy(pqn[:], ppv), [pp], [pqn])
                        else:
                            S.op("act", lambda e, pqn=pqn, ppv=ppv: e.copy(pqn[:, :, 1, :], ppv[:, :, 1, :]), [pp], [pqn])
                        yield
                        pz = psf.get()
                        pzv = pz[:, 0:256].rearrange("p (a e) -> p a e", a=2)
                        zn = ZZ[slot][lev % 2]
                        for (l, hp) in hps:
                            S.op("pe", lambda e, l=l, pqn=pqn, z=z: e.matmul(
                                pzv[:, l, :], lhsT=pqn[:, l, 1, :], rhs=z[:, l, :], start=True, stop=True), [pqn, z], [pz])
                        S.op("dve", lambda e, zn=zn, z=z, pzv=pzv: e.tensor_tensor(zn[:], pzv, z[:], ALU.add), [pz, z], [zn])
                        z = zn
                        pq = pqn
                        yield
                    pa = psf.get()
                    pav = pa[:, 0:128].rearrange("p (a e) -> p a e", a=2)
                    for (l, hp) in hps:
                        S.op("pe", lambda e, l=l, hp=hp: e.matmul(
                            pav[:, l, :], lhsT=ar[:, l, 0, :], rhs=vtok[:, c, hp, :], start=True, stop=True), [ar, vtok], [pa])
                    S.op("act", lambda e: e.copy(wk[:, :, 4, 0:64], pav), [pa], [wk])
                    yield
                    pg = psf.get()
                    pgv = pg[:, 0:384].rearrange("p (a e) -> p a e", a=2)
                    for (l, hp) in hps:
                        rhs = wk[:, l, :, :].rearrange("p s e -> p (s e)")[:, 384:576]
                        S.op("pe", lambda e, l=l, z=z, rhs=rhs: e.matmul(
                            pgv[:, l, :], lhsT=z[:, l, :], rhs=rhs, start=True, stop=True), [z, wk], [pg])
                    S.op("act", lambda e: e.copy(gh[:], pgv), [pg], [gh])
                    yield
                    pph = psf.get()
                    pphv = pph[:, :].rearrange("p (a s e) -> p a s e", a=2, s=2)
                    pps = psf.get()
                    ppsv = pps[:, 0:256].rearrange("p (a s e) -> p a s e", a=2, s=2)
                    for (l, hp) in hps:
                        rhs = wk[:, l, :, :].rearrange("p s e -> p (s e)")[:, 128:384]
                        S.op("pe", lambda e, l=l, rhs=rhs: e.matmul(
                            pphv[:, l, :, :].rearrange("p s e -> p (s e)"), lhsT=gh[:, l, 0:128], rhs=rhs, start=True, stop=True),
                            [gh, wk], [pph])
                    for (l, hp) in hps:
                        vv = vtok[:, c, hp, :]
                        hh_ = gh[:, l, 128:192]
                        S.op("pe", lambda e, l=l, vv=vv: e.matmul(
                            ppsv[:, l, 0, :], lhsT=wk[:, l, 5, :], rhs=vv, start=True, stop=False), [wk, vtok], [pps])
                        S.op("pe", lambda e, l=l, hh_=hh_: e.matmul(
                            ppsv[:, l, 0, :], lhsT=wk[:, l, 1, :], rhs=hh_, start=False, stop=True), [wk, gh], [pps])
                        S.op("pe", lambda e, l=l, vv=vv: e.matmul(
                            ppsv[:, l, 1, :], lhsT=ar[:, l, 1, :], rhs=vv, start=True, stop=False), [ar, vtok], [pps])
                        S.op("pe", lambda e, l=l, hh_=hh_: e.matmul(
                            ppsv[:, l, 1, :], lhsT=wk[:, l, 2, :], rhs=hh_, start=False, stop=True), [wk, gh], [pps])
                    S.op("pool", lambda e: e.tensor_tensor(
                        dw_[:], bc(identf[:].rearrange("p (o e) -> p o e", o=1), [128, 2, 128]),
                        bc(wtot[d][:, gs, c:c + 1], [128, 2, 128]), ALU.mult), [identf, wtot[d]], [dw_])
                    S.op("dve", lambda e: e.tensor_tensor(o1[:, :, 0, :], pphv[:, :, 0, :], dw_[:], ALU.add), [pph, dw_], [o1])
                    S.op("dve", lambda e: e.tensor_tensor(
                        o1[:, :, 1, :], pphv[:, :, 1, :], krt[d][:, gs, c, 1, :, :].rearrange("p a g t -> p a (g t)"), ALU.add),
                        [pph, krt[d]], [o1])
                    S.op("act", lambda e: e.copy(o2[:], ppsv), [pps], [o2])
                    S.dma(phiy_s[b, d, cg, :, g2 * 512:(g2 + 1) * 512], o1[:].rearrange("p a s e -> p (a s e)"), reads=[o1], q="sp")
                    S.dma(psiy_s[b, d, cg, :, g2 * 256:(g2 + 1) * 256], o2[:].rearrange("p a s e -> p (a s e)"), reads=[o2], q="pool")
                    yield

                all_sts = []
                for b in range(NB):
                    all_sts += [(b, True, t0_, t0_ == 0, t0_ + SN == L) for t0_ in range(0, L, SN)]
                    all_sts += [(b, False, t0_, t0_ == L, t0_ + SN == TT) for t0_ in range(L, TT, SN)]

                def load_pin(idx):
                    (b, is_ctx, tok0, left_zero, right_zero) = all_sts[idx]
                    P = pins[idx % 2]
                    n = SN
                    lo = tok0 - (0 if left_zero else 1)
                    hi = tok0 + n + (0 if right_zero else 1)
                    plo = 1 - (tok0 - lo)
                    grp = [("k", 0, 4), ("v", 512, 4), ("w", 1024, 1), ("a", 1152, 1)]
                    if not is_ctx:
                        grp.append(("r", 1280, 4))
                    for qi, (key, row0, nchn) in enumerate(grp):
                        t = P[key]
                        if nchn == 4:
                            if left_zero:
                                S.op("pool", lambda e, t=t: e.memset(t[:, :, 0:1], 0.0), [], [t])
                            if right_zero:
                                S.op("pool", lambda e, t=t: e.memset(t[:, :, n + 1:n + 2], 0.0), [], [t])
                            S.dma(t[:, :, plo:plo + (hi - lo)],
                                  rwT_s[b, row0:row0 + 512, lo:hi].rearrange("(a p) t -> p a t", p=128), writes=[t],
                                  q=("sp" if qi % 2 == 0 else "pool"))
                        else:
                            if left_zero:
                                S.op("pool", lambda e, t=t: e.memset(t[:, 0:1], 0.0), [], [t])
                            if right_zero:
                                S.op("pool", lambda e, t=t: e.memset(t[:, n + 1:n + 2], 0.0), [], [t])
                            S.dma(t[:, plo:plo + (hi - lo)], rwT_s[b, row0:row0 + 128, lo:hi], writes=[t],
                                  q=("sp" if qi % 2 == 0 else "pool"))

                def cw(ch0, k, nchn):
                    return bc(convT[:, ch0:ch0 + nchn, k:k + 1], [128, nchn, SN])

                load_pin(0)
                for idx, (b, is_ctx, tok0, left_zero, right_zero) in enumerate(all_sts):
                    if idx + 1 < len(all_sts):
                        load_pin(idx + 1)
                    P = pins[idx % 2]
                    n = SN
                    nchk = NCK
                    grp4 = [("k", u_k, 0), ("v", u_v, 4)] + ([] if is_ctx else [("r", u_r, 10)])
                    for gi_, (key, ug, ch0) in enumerate(grp4):
                        t = P[key]
                        eA, eB = ("dve", "pool") if gi_ % 2 == 0 else ("pool", "dve")
                        S.op(eA, lambda e, t=t, ug=ug, ch0=ch0: e.tensor_tensor(ug[:], t[:, :, 0:n], cw(ch0, 0, 4), ALU.mult), [t, convT], [ug])
                        S.op(eB, lambda e, t=t, ch0=ch0: e.tensor_tensor(ct[0][:], t[:, :, 1:n + 1], cw(ch0, 1, 4), ALU.mult), [t, convT], [ct[0]])
                        S.op(eA, lambda e, t=t, ch0=ch0: e.tensor_tensor(ct[1][:], t[:, :, 2:n + 2], cw(ch0, 2, 4), ALU.mult), [t, convT], [ct[1]])
                        S.op(eB, lambda e, ug=ug: e.tensor_tensor(ug[:], ug[:], ct[0][:], ALU.add), [ug, ct[0]], [ug])
                        S.op(eA, lambda e, ug=ug: e.tensor_tensor(ug[:], ug[:], ct[1][:], ALU.add), [ug, ct[1]], [ug])
                    if is_ctx:
                        S.op("pool", lambda e: e.memset(u_r[:], 0.0), [], [u_r])
                    for (key, ug, ch) in (("w", u_w, 8), ("a", u_a, 9)):
                        t = P[key]
                        S.op("act", lambda e, t=t, ug=ug, ch=ch: e.activation(
                            ug[:], t[:, 0:n], AF.Identity, scale=convT[:, ch, 0:1]), [t, convT], [ug])
                        S.op("dve", lambda e, t=t, ug=ug, ch=ch: e.scalar_tensor_tensor(
                            out=ug[:], in0=t[:, 1:n + 1], scalar=convT[:, ch, 1:2], in1=ug[:], op0=ALU.mult, op1=ALU.add), [t, convT, ug], [ug])
                        S.op("dve", lambda e, t=t, ug=ug, ch=ch: e.scalar_tensor_tensor(
                            out=ug[:], in0=t[:, 2:n + 2], scalar=convT[:, ch, 2:3], in1=ug[:], op0=ALU.mult, op1=ALU.add), [t, convT, ug], [ug])
                    S.op("act", lambda e: e.activation(th[:], u_w[:], AF.Tanh), [u_w], [th])
                    for d in range(2):
                        ds_ = slice(d * 64, (d + 1) * 64)
                        for hp in range(4):
                            pp = psf.get()
                            S.op("pe", lambda e, pp=pp, hp=hp, ds_=ds_: e.matmul(
                                pp[:, 0:n], lhsT=dw2[ds_, hp * 128:(hp + 1) * 128], rhs=th[ds_, :], start=True, stop=True), [dw2, th], [pp])
                            S.op("act", lambda e, pp=pp, hp=hp, d=d: e.activation(
                                sig[d][:, hp, :], pp[:, 0:n], AF.Sigmoid, bias=dw0T[:, d, hp:hp + 1]), [pp, dw0T], [sig[d]])
                            pp2 = psf.get()
                            S.op("pe", lambda e, pp2=pp2, hp=hp, ds_=ds_: e.matmul(
                                pp2[:, 0:n], lhsT=aw2[ds_, hp * 128:(hp + 1) * 128], rhs=u_a[ds_, :], start=True, stop=True), [aw2, u_a], [pp2])
                            S.op("act", lambda e, pp2=pp2, hp=hp, d=d: e.activation(
                                av[d][:, hp, :], pp2[:, 0:n], AF.Sigmoid, bias=a0T[:, d, hp:hp + 1]), [pp2, a0T], [av[d]])
                    kkb = bc(kkT[:].rearrange("p (a o) -> p a o", o=1), [128, 4, n])
                    S.op("dve", lambda e: e.tensor_tensor(kk0[:], u_k[:], kkb, ALU.mult), [u_k, kkT], [kk0])
                    S.op("pool", lambda e: e.tensor_tensor(sq[:], kk0[:], kk0[:], ALU.mult), [kk0], [sq])
                    S.op("pool", lambda e: e.tensor_copy(vb[:], u_v[:]), [u_v], [vb])
                    pp = psf.get()
                    for hp in range(4):
                        S.op("pe", lambda e, pp=pp, hp=hp: e.matmul(
                            pp[:, hp * n:(hp + 1) * n], lhsT=bones[:], rhs=sq[:, hp, :], start=True, stop=True), [bones, sq], [pp])
                    S.op("dve", lambda e, pp=pp: e.tensor_scalar(
                        rn[:].rearrange("p a t -> p (a t)"), pp[:, 0:4 * n], 1e-24, None, ALU.max), [pp], [rn])
                    S.op("act", lambda e: e.activation(rn[:], rn[:], AF.Sqrt), [rn], [rn])
                    S.op("dve", lambda e: e.reciprocal(rn[:], rn[:]), [rn], [rn])
                    S.op("dve", lambda e: e.tensor_tensor(kk[:], kk0[:], rn[:], ALU.mult), [kk0, rn], [kk])
                    for c in range(nchk):
                        pt = psb.get()
                        ptv = pt[:, 0:256].rearrange("p (a e) -> p a e", a=4)
                        for hp in range(4):
                            for hb in (0, 64):
                                S.op("pe", lambda e, ptv=ptv, hp=hp, hb=hb, c=c, pt=pt: e.transpose(
                                    ptv[hb:hb + 64, hp, :], vb[hb:hb + 64, hp, c * 64:(c + 1) * 64], identb[hb:hb + 64, hb:hb + 64]),
                                    [vb, identb], [pt])
                        S.op("act", lambda e, ptv=ptv, c=c, pt=pt: e.copy(vtok[:, c, :, :], ptv), [pt], [vtok])
                    kab = bc(kaT[:].rearrange("p (a o) -> p a o", o=1), [128, 4, n])
                    omb = bc(omka[:].rearrange("p (a o) -> p a o", o=1), [128, 4, n])
                    rkb = bc(rkT[:].rearrange("p (a o) -> p a o", o=1), [128, 4, n])
                    for d in range(2):
                        eA = "pool" if d == 0 else "dve"
                        S.op(eA, lambda e, d=d: e.tensor_tensor(tA[d][:], av[d][:], kab, ALU.mult), [av[d], kaT], [tA[d]])
                        S.op(eA, lambda e, d=d: e.tensor_tensor(tA[d][:], tA[d][:], omb, ALU.add), [tA[d], omka], [tA[d]])
                        S.op(eA, lambda e, d=d: e.tensor_tensor(kd[d][:], tA[d][:], u_k[:], ALU.mult), [tA[d], u_k], [kd[d]])
                        S.op("pool", lambda e, d=d: e.tensor_tensor(bb[d][:], kk[:], av[d][:], ALU.mult), [kk, av[d]], [bb[d]])
                    if not is_ctx:
                        pbn = psf.get()
                        for d in range(2):
                            S.op("pool", lambda e, d=d: e.tensor_tensor(tB[d][:], kd[d][:], u_r[:], ALU.mult), [kd[d], u_r], [tB[d]])
                            S.op("pool", lambda e, d=d: e.tensor_tensor(tB[d][:], tB[d][:], rkb, ALU.mult), [tB[d], rkT], [tB[d]])
                        for hp in range(4):
                            for d in range(2):
                                S.op("pe", lambda e, hp=hp, d=d: e.matmul(
                                    pbn[:, hp * n:(hp + 1) * n], lhsT=bones[:], rhs=tB[d][:, hp, :], start=(d == 0), stop=(d == 1)),
                                    [bones, tB[d]], [pbn])
                        S.op("dve", lambda e: e.tensor_tensor(
                            bonT[:].rearrange("p a t -> p (a t)"), pbn[:, 0:4 * n], u_v[:].rearrange("p a t -> p (a t)"), ALU.mult),
                            [pbn, u_v], [bonT])
                        for ti in range(n // 128):
                            pp = psf.get()
                            for hp in range(4):
                                S.op("pe", lambda e, pp=pp, hp=hp, ti=ti: e.transpose(
                                    pp[:, hp * 128:(hp + 1) * 128], bonT[:, hp, ti * 128:(ti + 1) * 128], identf[:]), [bonT, identf], [pp])
                            bs = bstg.get()
                            S.op("act", lambda e, bs=bs, pp=pp: e.copy(bs[:], pp[:, :]), [pp], [bs])
                            t_lat = tok0 - L + ti * 128
                            S.dma(bonus_s[b, t_lat:t_lat + 128, :], bs[:], reads=[bs], q="pool")
                    c4 = lambda ap: ap.rearrange("p a (c t) -> p a c t", t=64)
                    for d in range(2):
                        for hp in range(4):
                            S.op("dve", lambda e, hp=hp, d=d: e.tensor_tensor_scan(
                                Psg[d][:, hp, :], rmask[:, 0:n], sig[d][:, hp, :], 0.0, ALU.mult, ALU.add), [rmask, sig[d]], [Psg[d]])
                        P4 = c4(Psg[d][:])
                        tot_b = bc(P4[:, :, :, 63:64], [128, 4, nchk, 64])
                        S.op("pool", lambda e, d=d, P4=P4, tot_b=tot_b: e.tensor_tensor(c4(Qm[d][:]), tot_b, P4, ALU.subtract), [Psg[d]], [Qm[d]])
                        S.op("pool", lambda e, d=d: e.tensor_tensor(Em[d][:], Psg[d][:], sig[d][:], ALU.subtract), [Psg[d], sig[d]], [Em[d]])
                        if d == 0:
                            gi, ge, tg = Psg[d], Em[d], Qm[d]
                        else:
                            S.op("pool", lambda e, d=d: e.tensor_tensor(Gi[:], Qm[d][:], sig[d][:], ALU.add), [Qm[d], sig[d]], [Gi])
                            gi, ge, tg = Gi, Qm[d], Em[d]
                        X = ex[d]
                        S.op("act", lambda e, gi=gi, X=X: e.activation(X[0][:], gi[:], AF.Exp, scale=-C0), [gi], [X[0]])
                        S.op("act", lambda e, ge=ge, X=X: e.activation(X[1][:], ge[:], AF.Exp, scale=-C0), [ge], [X[1]])
                        S.op("act", lambda e, gi=gi, X=X: e.activation(X[2][:], gi[:], AF.Exp, scale=C0), [gi], [X[2]])
                        S.op("act", lambda e, tg=tg, X=X: e.activation(X[3][:], tg[:], AF.Exp, scale=-C0), [tg], [X[3]])
                        S.op("act", lambda e, d=d, P4=P4: e.activation(wtot[d][:], P4[:, :, :, 63], AF.Exp, scale=-C0), [Psg[d]], [wtot[d]])
                        for g in range(2):
                            bm = bmask[:, g:g + 1]
                            nbm = bmask[:, 2 + g:3 + g]
                            S.op("dve", lambda e, d=d, g=g, bm=bm, X=X: e.scalar_tensor_tensor(
                                out=krt[d][:, :, :, 0, g, :], in0=c4(kk[:]), scalar=bm, in1=c4(X[1][:]), op0=ALU.mult, op1=ALU.mult),
                                [kk, X[1], bmask], [krt[d]])
                            S.op("dve", lambda e, d=d, g=g, bm=bm, X=X: e.scalar_tensor_tensor(
                                out=krt[d][:, :, :, 1, g, :], in0=c4(u_r[:]), scalar=bm, in1=c4(X[0][:]), op0=ALU.mult, op1=ALU.mult),
                                [u_r, X[0], bmask], [krt[d]])
                            S.op("dve", lambda e, d=d, g=g, bm=bm, X=X: e.scalar_tensor_tensor(
                                out=kh[d][:, :, :, g, :], in0=c4(kd[d][:]), scalar=bm, in1=c4(X[2][:]), op0=ALU.mult, op1=ALU.mult),
                                [kd[d], X[2], bmask], [kh[d]])
                            S.op("dve", lambda e, d=d, g=g, bm=bm, X=X: e.scalar_tensor_tensor(
                                out=bh[d][:, :, :, g, :], in0=c4(bb[d][:]), scalar=bm, in1=c4(X[2][:]), op0=ALU.mult, op1=ALU.mult),
                                [bb[d], X[2], bmask], [bh[d]])
                            S.op("dve", lambda e, d=d, g=g, bm=bm, X=X: e.scalar_tensor_tensor(
                                out=kp[d][:, :, :, g, :], in0=c4(kd[d][:]), scalar=bm, in1=c4(X[3][:]), op0=ALU.mult, op1=ALU.mult),
                                [kd[d], X[3], bmask], [kp[d]])
                            S.op("dve", lambda e, d=d, g=g, nbm=nbm, X=X: e.scalar_tensor_tensor(
                                out=nbp[d][:, :, :, g, :], in0=c4(bb[d][:]), scalar=nbm, in1=c4(X[3][:]), op0=ALU.mult, op1=ALU.mult),
                                [bb[d], X[3], bmask], [nbp[d]])
                    jobs = [(c, d, g2) for c in range(nchk) for g2 in range(2) for d in range(2)]
                    for j0 in range(0, len(jobs), NCH):
                        gens = [chain(si, b, d, c, tok0 // 64 + c, g2) for si, (c, d, g2) in enumerate(jobs[j0:j0 + NCH])]
                        alive = list(gens)
                        while alive:
                            nxt = []
                            for g in alive:
                                try:
                                    next(g)
                                    nxt.append(g)
                                except StopIteration:
                                    pass
                            alive = nxt
                S.barrier()

        if "s2b" in stages:
            with ExitStack() as st:
                A_r = SbRing(S, st, 6, [128, 4, 2, 128], BF16, "Aphi")
                B_r = SbRing(S, st, 6, [128, 2, 2, 2, 64], F32, "Bpsi")
                ST = [[S.sb(st, [128, 4, 64], BF16, "ST") for _ in range(2)] for _ in range(2)]
                yo_r = SbRing(S, st, 4, [128, 4, 64], F32, "yo")
                order = [list(range(4)) + list(range(4, NCHUNK)),
                         list(range(3, -1, -1)) + list(range(NCHUNK - 1, 3, -1))]
                for b in range(NB):
                    for d in range(2):
                        S.op("pool", lambda e, d=d: e.memset(ST[d][0][:], 0.0), [], [ST[d][0]])
                    for n_ in range(NCHUNK):
                        for d in range(2):
                            cg = order[d][n_]
                            A = A_r.get(); Bm = B_r.get()
                            S.dma(A[:].rearrange("p a s e -> p (a s e)"), phiy_s[b, d, cg, :, :], writes=[A], q="sp")
                            S.dma(Bm[:].rearrange("p g a s e -> p (g a s e)"), psiy_s[b, d, cg, :, :], writes=[Bm], q="pool")
                            Bv = Bm[:].rearrange("p g a s e -> p (g a) s e")
                            cur = ST[d][n_ % 2]; new = ST[d][(n_ + 1) % 2]
                            pS = psf.get()
                            pSv = pS[:, 0:512].rearrange("p (s a e) -> p s a e", s=2, a=4)
                            for hp in range(4):
                                S.op("pe", lambda e, hp=hp, A=A, cur=cur, pSv=pSv, pS=pS: e.matmul(
                                    pSv[:, 0, hp, :], lhsT=A[:, hp, 0, :], rhs=cur[:, hp, :], start=True, stop=True), [A, cur], [pS])
                                if cg >= 4:
                                    S.op("pe", lambda e, hp=hp, A=A, cur=cur, pSv=pSv, pS=pS: e.matmul(
                                        pSv[:, 1, hp, :], lhsT=A[:, hp, 1, :], rhs=cur[:, hp, :], start=True, stop=True), [A, cur], [pS])
                            S.op("dve", lambda e, new=new, pSv=pSv, Bv=Bv: e.tensor_tensor(
                                new[:], pSv[:, 0, :, :], Bv[:, :, 0, :], ALU.add), [pS, Bm], [new])
                            if cg >= 4:
                                yo = yo_r.get()
                                S.op("dve", lambda e, yo=yo, pSv=pSv, Bv=Bv: e.tensor_tensor(
                                    yo[:], pSv[:, 1, :, :], Bv[:, :, 1, :], ALU.add), [pS, Bm], [yo])
                                t0 = (cg - 4) * 64
                                dst = yd_s[b, d, t0:t0 + 64, :].rearrange("t (a h e) -> t a h e", a=4, h=2)
                                for h2 in range(2):
                                    S.dma(dst[:, :, h2, :], yo[h2 * 64:(h2 + 1) * 64, :, :], reads=[yo],
                                          q=("sp" if h2 == 0 else "pool"))
                S.barrier()

        if "s4" in stages:
            with ExitStack() as st:
                gnw = S.sb(st, [128, 512], F32, "gnw"); gnb = S.sb(st, [128, 512], F32, "gnb")
                fg = S.sb(st, [128, D], F32, "fg")
                S.dma(gnw[:], gnw_d[:, :], writes=[gnw]); S.dma(gnb[:], gnb_d[:, :], writes=[gnb])
                S.dma(fg[:], fg_d[:, :], writes=[fg])
                xt_r = SbRing(S, st, 2, [128, D], F32, "xt4")
                mix_r = SbRing(S, st, 2, [128, D], F32, "mix")
                y0_r = SbRing(S, st, 2, [128, 8, 64], F32, "y0")
                y1_r = SbRing(S, st, 2, [128, 8, 64], F32, "y1")
                bo_r = SbRing(S, st, 2, [128, 512], F32, "bo")
                sg_r = SbRing(S, st, 2, [128, 512], F32, "sg4")
                sq_r = SbRing(S, st, 2, [128, 8, 64], F32, "sq")
                st8_r = SbRing(S, st, 4, [128, 8, 1], F32, "st8")
                mixT_r = SbRing(S, st, 2, [128, 8, 128], BF16, "mixT")
                hn_r = SbRing(S, st, 2, [128, D], F32, "hn")
                junk = S.sb(st, [128, D], F32, "junk4")
                s1_r = SbRing(S, st, 4, [128, 1], F32, "s1")
                ob_r = SbRing(S, st, 2, [128, D], F32, "ob")
                woutf = S.sb(st, [128, 8, D], F32, "woutf")
                woutg1 = S.sb(st, [128, 8, D], BF16, "woutg")
                S.dma(woutf[:], wout_d.rearrange("(kc p) n -> p kc n", p=128), writes=[woutf], q="pool")
                for b in range(NB):
                    S.op("dve", lambda e, b=b: e.tensor_tensor(
                        woutg1[:], woutf[:], bc(gbc[b][:].rearrange("p (o n) -> p o n", o=1), [128, 8, D]), ALU.mult),
                        [woutf, gbc[b]], [woutg1])
                    for ti in range(T // 128):
                        ts_ = slice(ti * 128, (ti + 1) * 128)
                        xt = xt_r.get(); mix = mix_r.get(); y0 = y0_r.get(); y1 = y1_r.get(); bo = bo_r.get(); sg = sg_r.get()
                        S.dma(xt[:], x_d[b, ts_, :], writes=[xt])
                        S.dma(mix[:, 0:512], nag_s[b, ts_, :], writes=[mix], q="pool")
                        S.dma(y0[:].rearrange("p h e -> p (h e)"), yd_s[b, 0, ts_, :], writes=[y0])
                        S.dma(y1[:].rearrange("p h e -> p (h e)"), yd_s[b, 1, ts_, :], writes=[y1], q="pool")
                        S.dma(bo[:], bonus_s[b, ts_, :], writes=[bo])
                        S.dma(sg[:], sgrw_s[b, ts_, :], writes=[sg], q="pool")
                        S.op("pool", lambda e, y0=y0, y1=y1: e.tensor_tensor(y0[:], y0[:], y1[:], ALU.add), [y0, y1], [y0])
                        mu = st8_r.get(); var = st8_r.get(); sq = sq_r.get()
                        S.op("dve", lambda e, mu=mu, y0=y0: e.tensor_reduce(out=mu[:, :, 0], in_=y0[:], axis=AX.X, op=ALU.add), [y0], [mu])
                        S.op("dve", lambda e, mu=mu: e.tensor_scalar(mu[:], mu[:], -1.0 / 64, None, ALU.mult), [mu], [mu])
                        S.op("dve", lambda e, mu=mu, y0=y0: e.tensor_tensor(y0[:], y0[:], bc(mu[:], [128, 8, 64]), ALU.add), [y0, mu], [y0])
                        S.op("pool", lambda e, sq=sq, y0=y0: e.tensor_tensor(sq[:], y0[:], y0[:], ALU.mult), [y0], [sq])
                        S.op("dve", lambda e, var=var, sq=sq: e.tensor_reduce(out=var[:, :, 0], in_=sq[:], axis=AX.X, op=ALU.add), [sq], [var])
                        S.op("act", lambda e, var=var: e.activation(var[:], var[:], AF.Sqrt, bias=GN_EPS, scale=1.0 / 64), [var], [var])
                        S.op("dve", lambda e, var=var: e.reciprocal(var[:], var[:]), [var], [var])
                        S.op("dve", lambda e, var=var, y0=y0: e.tensor_tensor(y0[:], y0[:], bc(var[:], [128, 8, 64]), ALU.mult), [y0, var], [y0])
                        yf = y0[:].rearrange("p h e -> p (h e)")
                        S.op("pool", lambda e, yf=yf, y0=y0: e.tensor_tensor(yf, yf, gnw[:], ALU.mult), [y0, gnw], [y0])
                        S.op("pool", lambda e, yf=yf, y0=y0: e.tensor_tensor(yf, yf, gnb[:], ALU.add), [y0, gnb], [y0])
                        S.op("pool", lambda e, yf=yf, y0=y0, bo=bo: e.tensor_tensor(yf, yf, bo[:], ALU.add), [y0, bo], [y0])
                        S.op("pool", lambda e, yf=yf, y0=y0, sg=sg, mix=mix: e.tensor_tensor(mix[:, 512:1024], yf, sg[:], ALU.mult), [y0, sg], [mix])
                        mixT = mixT_r.get()
                        for half in range(2):
                            pt = psf.get()
                            for j in range(4):
                                fc = half * 4 + j
                                S.op("pe", lambda e, pt=pt, j=j, fc=fc, mix=mix: e.transpose(
                                    pt[:, j * 128:(j + 1) * 128], mix[:, fc * 128:(fc + 1) * 128], identf[:]), [mix, identf], [pt])
                            if half == 0:
                                S.op("act", lambda e, pt=pt, mixT=mixT: e.copy(
                                    mixT[:, 0:4, :], pt[:, :].rearrange("p (a t) -> p a t", a=4)), [pt], [mixT])
                            else:
                                S.op("dve", lambda e, pt=pt, mixT=mixT: e.tensor_copy(
                                    mixT[:, 4:8, :], pt[:, :].rearrange("p (a t) -> p a t", a=4)), [pt], [mixT])
                        hn = hn_r.get()
                        for nh in range(2):
                            pp = psf.get()
                            for fc in range(8):
                                S.op("pe", lambda e, pp=pp, fc=fc, nh=nh, mixT=mixT: e.matmul(
                                    pp[:, :], lhsT=mixT[:, fc, :], rhs=woutg1[:, fc, nh * 512:(nh + 1) * 512],
                                    start=(fc == 0), stop=(fc == 7)), [mixT, woutg1], [pp])
                            S.op("dve", lambda e, pp=pp, nh=nh, hn=hn, xt=xt: e.tensor_tensor(
                                hn[:, nh * 512:(nh + 1) * 512], pp[:, :], xt[:, nh * 512:(nh + 1) * 512], ALU.add), [pp, xt], [hn])
                        s1 = s1_r.get(); ob = ob_r.get()
                        S.op("act", lambda e, hn=hn, s1=s1: e.activation(junk[:], hn[:], AF.Square, accum_out=s1[:]), [hn], [junk, s1])
                        S.op("act", lambda e, s1=s1: e.activation(s1[:], s1[:], AF.Sqrt, bias=RMS_EPS, scale=1.0 / D), [s1], [s1])
                        S.op("dve", lambda e, s1=s1: e.reciprocal(s1[:], s1[:]), [s1], [s1])
                        S.op("dve", lambda e, hn=hn, s1=s1, ob=ob: e.scalar_tensor_tensor(
                            out=ob[:], in0=hn[:], scalar=s1[:, 0:1], in1=fg[:], op0=ALU.mult, op1=ALU.mult), [hn, s1, fg], [ob])
                        S.dma(out_d[b, ts_, :], ob[:], reads=[ob], q="sp")
                S.barrier()
        S.barrier()
    return nc


def _const_tables():
    idx = np.arange(64)
    mt_strict = [(idx[:, None] < idx[None, :]), (idx[:, None] > idx[None, :])]
    mt_le = [(idx[:, None] <= idx[None, :]), (idx[:, None] >= idx[None, :])]
    m_strict = [(idx[None, :] < idx[:, None]), (idx[None, :] > idx[:, None])]
    m1 = np.zeros((64, 2, 2, 64), np.float32)
    m2 = np.zeros((64, 2, 2, 64), np.float32)
    m3 = np.zeros((64, 2, 64), np.float32)
    for d in range(2):
        m1[:, d, 0, :] = mt_strict[d]
        m1[:, d, 1, :] = -mt_le[d].astype(np.float32)
        m2[:, d, 0, :] = mt_strict[d]
        m2[:, d, 1, :] = mt_le[d]
        m3[:, d, :] = m_strict[d]
    dup = lambda a: np.ascontiguousarray(np.concatenate([a, a], axis=0))
    dupl = lambda a: np.ascontiguousarray(np.concatenate([a, a], axis=-1))
    bmask = np.zeros((128, 4), np.float32)
    bmask[:64, 0] = 1.0; bmask[64:, 1] = 1.0; bmask[:64, 2] = -1.0; bmask[64:, 3] = -1.0
    bones = np.zeros((128, 128), np.float32)
    bones[:64, :64] = 1.0
    bones[64:, 64:] = 1.0
    rmask = np.ones((128, 512), np.float32)
    rmask[:, 0::64] = 0.0
    identf = np.eye(128, dtype=np.float32)
    sel = np.zeros((3, NB, 128), np.float32)
    for b in range(NB):
        sel[b, b, :] = 1.0
    return dict(m1e=dupl(dup(m1)), m2e=dupl(dup(m2)), m3e=dupl(dup(m3)), bmask=bmask, bones=bones, rmask=rmask, identf=identf, sel=sel)


def _bias_table(rpb):
    c = np.arange(64)
    j = np.arange(64)
    c0 = np.clip(j - 8, 0, 48)
    inwin = (c[:, None] >= c0[None, :]) & (c[:, None] < c0[None, :] + 16)
    coff = np.clip(c[:, None] - j[None, :] + 15, 0, 30)
    tb = np.full((2, 64, 8, 14, 64), -30000.0, np.float32)
    for rr in range(2):
        for rho0 in range(14):
            g = rpb[:, rho0 + rr, :][:, coff]
            tb[rr, :, :, rho0, :] = np.where(inwin[:, None, :], np.transpose(g, (1, 0, 2)), np.float32(-30000.0))
    return np.ascontiguousarray(tb.reshape(128, 8, 14, 64))


def make_in_maps(inp):
    f = lambda a: np.ascontiguousarray(np.asarray(a, dtype=np.float32))
    x = f(inp["x"]); c = f(inp["c"]); ctx = f(inp["ctx"]); c_ctx = f(inp["c_ctx"])
    w_mod = f(inp["w_mod"])[0]; b_mod = f(inp["b_mod"])[0]; norm_g = f(inp["norm_g"])[0]
    w_in = f(inp["w_in"])[0]; conv_w = f(inp["conv_w"])[0]
    dw0 = f(inp["decay_w0"])[0]; dw2 = f(inp["decay_w2"])[0]; a0 = f(inp["aaa_a0"])[0]; a2 = f(inp["aaa_a2"])[0]
    k_k = f(inp["k_k"])[0]; k_a = f(inp["k_a"])[0]; r_k = f(inp["r_k"])[0].reshape(512)
    gn_w = f(inp["gn_w"])[0]; gn_b = f(inp["gn_b"])[0]; rpb = f(inp["na_rpb"])[0]
    w_out = f(inp["w_out"])[0]; final_g = f(inp["final_g"])
    consts = _const_tables()
    colT = lambda v: np.ascontiguousarray(v.reshape(-1, 128).T)
    shared = dict(
        w_mod=w_mod, bmodT=colT(b_mod), bgate=np.ascontiguousarray(np.broadcast_to(b_mod[2 * D:], (3, D))),
        normgT=colT(norm_g), w_in=w_in,
        convT=np.ascontiguousarray(conv_w.reshape(3, 14, 128).transpose(2, 1, 0)),
        dw0T=np.ascontiguousarray(dw0.reshape(2, 4, 128).transpose(2, 0, 1)),
        a0T=np.ascontiguousarray(a0.reshape(2, 4, 128).transpose(2, 0, 1)),
        dw2=np.ascontiguousarray(dw2.reshape(128, 512)), aw2=np.ascontiguousarray(a2.reshape(128, 512)),
        kkT=colT(k_k), kaT=colT(k_a), rkT=colT(r_k),
        gnw_bc=np.ascontiguousarray(np.broadcast_to(gn_w, (128, 512))),
        gnb_bc=np.ascontiguousarray(np.broadcast_to(gn_b, (128, 512))),
        fg_bc=np.ascontiguousarray(np.broadcast_to(final_g, (128, D))),
        w_out=w_out, tb=_bias_table(rpb), **consts)
    maps = []
    for core in range(8):
        b0 = core * NB
        cv = np.stack([c[b0], c[b0 + 1], c_ctx], axis=0)
        cT = np.ascontiguousarray(cv.reshape(3, 8, 128).transpose(2, 1, 0))
        m = dict(shared)
        m.update(x=np.ascontiguousarray(x[b0:b0 + NB]), ctx=np.ascontiguousarray(ctx[b0:b0 + NB]), cT=cT)
        maps.append(m)
    return maps


def kernel(**inputs):
    nc = build_program()
    maps = make_in_maps(inputs)
    res = run_bass_kernel_spmd(nc, maps, core_ids=list(range(8)))
    out = np.concatenate([np.asarray(r["out"], dtype=np.float32) for r in res.results], axis=0)
    return out
```
